# Optimizing a Trainium2 kernel written in Bass

```python
import jax, jax.numpy as jnp
from jax import lax
import numpy as np

D_MODEL = 2048
BATCH = 2
SEQ = 4096
DEPTH = 1

MIX_WIDTH = D_MODEL
FOURIER_HEAD_DIM = 128
FOURIER_HEADS = (MIX_WIDTH // 2) // FOURIER_HEAD_DIM
FOURIER_WIDTH = FOURIER_HEADS * FOURIER_HEAD_DIM
GMLP_HEAD_DIM = 128
GMLP_HEADS = (MIX_WIDTH - FOURIER_WIDTH) // GMLP_HEAD_DIM
GMLP_WIDTH = GMLP_HEADS * GMLP_HEAD_DIM
CHUNK = 128
IN_WIDTH = FOURIER_WIDTH + 2 * GMLP_WIDTH
D_FF = (-(-8 * D_MODEL // (3 * 256))) * 256
EPS = 1e-6

kernel_name = "hybrid_fnet_gmlp_encoder_block"


def rmsnorm(x, g):
    xf = x.astype(jnp.float32)
    y = xf * lax.rsqrt(jnp.mean(xf * xf, axis=-1, keepdims=True) + EPS)
    return (y * g.astype(jnp.float32)).astype(x.dtype)


def fourier_mixer(a, w_fourier):
    b, s, _ = a.shape
    a4 = a.reshape(b, s, FOURIER_HEADS, FOURIER_HEAD_DIM).astype(jnp.float32)
    f = jnp.fft.fftn(a4, axes=(1, 3), norm="ortho").real.astype(a.dtype)
    f = jnp.einsum("bshd,hde->bshe", f, w_fourier)
    return f.reshape(b, s, FOURIER_WIDTH)


def spatial_gating_mixer(u, v, g_sgu, w_spatial, b_spatial):
    b, s, _ = v.shape
    v = rmsnorm(v, g_sgu)
    v5 = v.reshape(b, s // CHUNK, CHUNK, GMLP_HEADS, GMLP_HEAD_DIM)
    sv = jnp.einsum("hpq,bnqhd->bnphd", w_spatial, v5)
    sv = sv + jnp.transpose(b_spatial)[None, None, :, :, None]
    return u * sv.reshape(b, s, GMLP_WIDTH)


def setup_inputs(seed: int = 0) -> dict:
    key = jax.random.key(seed)
    ks = jax.random.split(key, 16)
    f32 = jnp.float32
    L = DEPTH
    x = jax.random.normal(ks[0], (BATCH, SEQ, D_MODEL), f32)
    norm_mix = 1.0 + 0.02 * jax.random.normal(ks[1], (L, D_MODEL), f32)
    w_in = jax.random.normal(ks[2], (L, D_MODEL, IN_WIDTH), f32) * D_MODEL ** -0.5
    w_fourier = jax.random.normal(ks[3], (L, FOURIER_HEADS, FOURIER_HEAD_DIM, FOURIER_HEAD_DIM), f32) * FOURIER_HEAD_DIM ** -0.5
    sgu_norm = 1.0 + 0.02 * jax.random.normal(ks[4], (L, GMLP_WIDTH), f32)
    w_spatial = jax.random.normal(ks[5], (L, GMLP_HEADS, CHUNK, CHUNK), f32) * CHUNK ** -0.5
    b_spatial = 1.0 + 0.02 * jax.random.normal(ks[6], (L, GMLP_HEADS, CHUNK), f32)
    w_out = jax.random.normal(ks[7], (L, MIX_WIDTH, D_MODEL), f32) * MIX_WIDTH ** -0.5
    norm_ffn = 1.0 + 0.02 * jax.random.normal(ks[8], (L, D_MODEL), f32)
    w_gate = jax.random.normal(ks[9], (L, D_MODEL, D_FF), f32) * D_MODEL ** -0.5
    w_up = jax.random.normal(ks[10], (L, D_MODEL, D_FF), f32) * D_MODEL ** -0.5
    w_down = jax.random.normal(ks[11], (L, D_FF, D_MODEL), f32) * D_FF ** -0.5
    norm_final = 1.0 + 0.02 * jax.random.normal(ks[12], (D_MODEL,), f32)
    return {"x": x, "norm_mix": norm_mix, "w_in": w_in, "w_fourier": w_fourier,
            "sgu_norm": sgu_norm, "w_spatial": w_spatial, "b_spatial": b_spatial,
            "w_out": w_out, "norm_ffn": norm_ffn, "w_gate": w_gate, "w_up": w_up,
            "w_down": w_down, "norm_final": norm_final}


def reference(x, norm_mix, w_in, w_fourier, sgu_norm, w_spatial, b_spatial, w_out,
              norm_ffn, w_gate, w_up, w_down, norm_final):
    for l in range(DEPTH):
        h = rmsnorm(x, norm_mix[l])
        z = h @ w_in[l]
        a = z[..., :FOURIER_WIDTH]
        uv = jax.nn.gelu(z[..., FOURIER_WIDTH:])
        u = uv[..., :GMLP_WIDTH]
        v = uv[..., GMLP_WIDTH:]
        y_f = fourier_mixer(a, w_fourier[l])
        y_g = spatial_gating_mixer(u, v, sgu_norm[l], w_spatial[l], b_spatial[l])
        x = x + jnp.concatenate([y_f, y_g], axis=-1) @ w_out[l]
        h2 = rmsnorm(x, norm_ffn[l])
        x = x + (jax.nn.silu(h2 @ w_gate[l]) * (h2 @ w_up[l])) @ w_down[l]
    return rmsnorm(x, norm_final)
```

```python
import numpy as np
import ml_dtypes
import concourse.bass as bass
import concourse.mybir as mybir
from concourse.bass_utils import run_bass_kernel_spmd

F32 = mybir.dt.float32
BF16 = mybir.dt.bfloat16
AF = mybir.ActivationFunctionType
ALU = mybir.AluOpType

NCORES = 8
P = 128
D = 2048
DC = 16
SEQ = 4096
TOK = 1024
NT = 8
TB = 512
DFF = 5632
NFC = 44
NQ = 4
FQ = 11
EPS = 1e-6
KB = 1024


class Sem:
    def __init__(self, h):
        self.h = h
        self.n = 0


class EngQ:
    def __init__(self, name):
        self.name = name
        self.ops = []
        self.sem = None
        self.waited = {}

    def wait(self, tok):
        if tok is None:
            return
        sem, val = tok
        if self.waited.get(id(sem), 0) >= val:
            return
        self.waited[id(sem)] = val
        self.ops.append(lambda e: e.wait_ge(sem.h, val))

    def op(self, fn, waits=(), sig=False, inc=None):
        for w in waits:
            self.wait(w)
        if sig:
            sem = self.sem
            sem.n += 1
            v = sem.n
            self.ops.append(lambda e: fn(e).then_inc(sem.h, 1))
            return (sem, v)
        if inc is not None:
            inc.n += 16
            v = inc.n
            self.ops.append(lambda e: fn(e).then_inc(inc.h, 16))
            return (inc, v)
        self.ops.append(fn)
        return None

    def run(self, e):
        for f in self.ops:
            f(e)


class Banks:
    def __init__(self):
        self.next = 0
        self.free = [[] for _ in range(8)]

    def alloc(self, n=1):
        if n == 2 and self.next % 2:
            self.next = (self.next + 1) % 8
        b = self.next
        self.next = (self.next + n) % 8
        toks = []
        for i in range(n):
            toks += self.free[b + i]
            self.free[b + i] = []
        return b, toks

    def release(self, b, tok, n=1):
        for i in range(n):
            self.free[b + i].append(tok)


def build_nc():
    nc = bass.Bass("TRN2", target_bir_lowering=False)

    def din(name, shape, dt=F32):
        return nc.dram_tensor(name, list(shape), dt, kind="ExternalInput").ap()

    x_d = din("x", [TOK, D])
    gmix_d = din("gmix", [P, D])
    gffn_d = din("gffn", [P, D])
    gfin_d = din("gfin", [P, D])
    wa_d = din("wa", [8, P, DC, 128])
    wu_d = din("wu", [8, P, DC, 128])
    wv_d = din("wv", [P, DC, 1024])
    wo_d = din("wo", [4, P, DC, 512])
    wg_d = din("wg", [NFC, P, DC, 128])
    wup_d = din("wup", [NFC, P, DC, 128])
    wd_d = din("wd", [NQ, 4, P, FQ, 512])
    wf_d = din("wf", [P, 8, 128])
    wst_d = din("wst", [P, 8, 128])
    bb_d = din("bb", [P, 8, 128])
    sg_d = din("sg", [P, 8])
    ctab_d = din("ctab", [P, DC, 512], BF16)
    nstab_d = din("nstab", [P, DC, 512], BF16)
    sgn_d = din("sgn", [1, 512], BF16)
    cd_d = din("cd", [P, 128], BF16)
    sd_d = din("sd", [P, 128], BF16)
    id_d = din("ident", [P, 128], BF16)
    out_d = nc.dram_tensor("out", [TOK, D], F32, kind="ExternalOutput").ap()
    ib = nc.dram_tensor("ib", [1024, TOK], BF16)
    ob = nc.dram_tensor("ob", [NCORES * 1024, TOK], BF16)

    ARENA = 200 * KB
    n_sems = 40
    from contextlib import ExitStack
    with ExitStack() as es:
        arena = es.enter_context(nc.sbuf_tensor("arena", [P, ARENA // 2], BF16))
        psum = es.enter_context(nc.psum_tensor("psum", [P, 4096], F32))
        sems = [Sem(es.enter_context(nc.semaphore(f"s{i}"))) for i in range(n_sems)]
        block = es.enter_context(nc.Block())
        sem_it = iter(sems)

        def newsem():
            return next(sem_it)

        PE, ACT, DVE, POOL, SP = EngQ("pe"), EngQ("act"), EngQ("dve"), EngQ("pool"), EngQ("sp")
        PE.sem, ACT.sem, DVE.sem = newsem(), newsem(), newsem()
        cc_sem = newsem()

        def vb(off, shape):
            n = int(np.prod(shape))
            a = arena[:, off // 2: off // 2 + n]
            if len(shape) == 2:
                a = a.rearrange("p (a b) -> p a b", a=shape[0])
            elif len(shape) == 3:
                a = a.rearrange("p (a b c) -> p a b c", a=shape[0], b=shape[1])
            return a

        def vf(off, shape):
            n = int(np.prod(shape))
            a = arena[:, off // 2: off // 2 + 2 * n].bitcast(F32)
            if len(shape) == 2:
                a = a.rearrange("p (a b) -> p a b", a=shape[0])
            return a

        def bank(b, n=1):
            return psum[:, b * 512:(b + n) * 512]

        def bank_bf(b, n=1):
            return psum[:, b * 512:(b + n) * 512].bitcast(BF16)

        hT = vb(0, [DC, TOK])
        uT = vb(32 * KB, [8, TOK])
        aT = vb(48 * KB, [8, TOK])
        vn = vb(48 * KB, [NT, 1024])
        x1 = vf(0, [NT, D])
        wfb = vb(32 * KB, [8, 128])
        xt = [vf(64 * KB + i * 8 * KB, [D]) for i in range(2)]
        ht = [vb(80 * KB + i * 4 * KB, [D]) for i in range(2)]
        vfs = [vf(64 * KB + i * 4 * KB, [1024]) for i in range(2)]
        sptmp = [vf(72 * KB + i * 2 * KB, [512]) for i in range(2)]
        yT = vb(64 * KB, [16, TOK])
        wd_buf = [vb(64 * KB + i * 11 * KB, [FQ, 512]) for i in range(2)]
        hidA = vb(86 * KB, [5, TOK])
        ctab = vb(96 * KB, [DC, 512])
        nstab = vb(112 * KB, [DC, 512])
        h2T = vb(96 * KB, [DC, TOK])
        junkC = vb(96 * KB, [D])
        au_buf = [vb(128 * KB + i * 4 * KB, [DC, 128]) for i in range(4)]
        wv = vb(144 * KB, [DC, 1024])
        abh = vb(128 * KB, [4096])
        apb = vb(136 * KB, [2048])
        amb = vb(140 * KB, [2048])
        Bp = [vb(144 * KB + i * 8 * KB, [DC, 128]) for i in range(2)]
        Bm = [vb(148 * KB + i * 8 * KB, [DC, 128]) for i in range(2)]
        wo_buf = [vb(160 * KB, [DC, 512]), vb(128 * KB, [DC, 512])]
        h2t = [vb(144 * KB + i * 4 * KB, [D]) for i in range(2)]
        gu_buf = [vb(128 * KB + i * 4 * KB, [DC, 128]) for i in range(8)]
        hidB = vb(160 * KB, [6, TOK])
        sgtmp = [vf(172 * KB + i * 2 * KB, [512]) for i in range(2)]
        gb = vf(176 * KB, [D])
        o = 184 * KB
        ident = vb(o, [128]); o += 256
        cdt = vb(o, [128]); o += 256
        sdt = vb(o, [128]); o += 256
        Gflat = vb(o, [2048]); G = vb(o, [8, 2, 128]); o += 4096
        wst = vb(o, [8, 128]); o += 2048
        bbt = vf(o, [8, 128]); o += 4096
        sgt = vf(o, [8]); o += 32
        sgn = vb(o, [512]); o += 1024
        bhalf = [vb(o + i * 256, [128]) for i in range(2)]; o += 512
        stat = vf(o, [64]); o += 256
        epsb = vf(o, [1]); o += 32
        assert o <= ARENA

        def hid(f):
            return hidA[:, f, :] if f < 5 else hidB[:, f - 5, :]

        banks = Banks()

        def MM(out, lhsT, rhs, start, stop, waits=(), sig=False):
            return PE.op(lambda e: e.matmul(out, lhsT, rhs, start=start, stop=stop), waits, sig)

        def TR(out, in_, waits=(), sig=False):
            return PE.op(lambda e: e.transpose(out, in_, ident), waits, sig)

        def ACTF(out, in_, func, waits=(), sig=True, bias=None, scale=None, accum=None):
            kw = {}
            if bias is not None:
                kw["bias"] = bias
            if scale is not None:
                kw["scale"] = scale
            if accum is not None:
                kw["accum_out"] = accum
            return ACT.op(lambda e: e.activation(out, in_, func, **kw), waits, sig)

        def ACOPY(out, in_, waits=(), sig=True):
            return ACT.op(lambda e: e.copy(out, in_), waits, sig)

        def VCOPY(out, in_, waits=(), sig=True):
            return DVE.op(lambda e: e.tensor_copy(out, in_), waits, sig)

        def VTT(out, in0, in1, op, waits=(), sig=True):
            return DVE.op(lambda e: e.tensor_tensor(out, in0, in1, op), waits, sig)

        def VTS(out, in0, s1, s2, op0, op1=None, waits=(), sig=True):
            if op1 is None:
                return DVE.op(lambda e: e.tensor_scalar(out, in0, s1, None, op0), waits, sig)
            return DVE.op(lambda e: e.tensor_scalar(out, in0, s1, s2, op0, op1), waits, sig)

        def VSTT(out, in0, scalar, in1, op0, op1, waits=(), sig=True):
            return DVE.op(lambda e: e.scalar_tensor_tensor(out, in0, scalar, in1, op0, op1), waits, sig)

        def VRECIP(out, in_, waits=(), sig=True):
            return DVE.op(lambda e: e.reciprocal(out, in_), waits, sig)

        def DMA(q, out, in_, sem, waits=()):
            return q.op(lambda e: e.dma_start(out=out, in_=in_), waits, inc=sem)

        def rstd_chain(ss_ap, sd_ap, r_ap, n, tok_ss):
            t1 = ACTF(sd_ap, ss_ap, AF.Sqrt, waits=[tok_ss], bias=epsb[:, 0:1], scale=1.0 / n)
            return VRECIP(r_ap, sd_ap, waits=[t1])

        evac_flip = [0]

        def evac_copy(out, in_, waits):
            evac_flip[0] ^= 1
            if evac_flip[0]:
                return ACOPY(out, in_, waits)
            return VCOPY(out, in_, waits)

        s_const = newsem()
        for dst, src in ((ident, id_d), (cdt, cd_d), (sdt, sd_d), (bbt, bb_d), (sgt, sg_d), (gb, gmix_d)):
            t_const = DMA(SP, dst, src, s_const)
        t_const = DMA(SP, sgn[0:1, :], sgn_d, s_const)
        s_tab = newsem()
        DMA(SP, ctab, ctab_d, s_tab)
        t_tab = DMA(SP, nstab, nstab_d, s_tab)
        s_pc = newsem()
        DMA(POOL, wfb, wf_d, s_pc)
        t_pc = DMA(POOL, wst, wst_d, s_pc)
        t_eps = DVE.op(lambda e: e.memset(epsb, EPS), sig=True)

        s_au = [newsem() for _ in range(4)]
        au_tok = {}
        au_free = {}
        def load_au(i):
            src = wa_d[i] if i < 8 else wu_d[i - 8]
            waits = [au_free[i - 4]] if i >= 4 else []
            au_tok[i] = DMA(POOL, au_buf[i % 4], src, s_au[i % 4], waits)
        for i in range(4):
            load_au(i)
        s_wv = newsem()
        for kq in range(4):
            t_wv = DMA(POOL, wv[:, kq * 4:(kq + 1) * 4, :], wv_d[:, kq * 4:(kq + 1) * 4, :], s_wv)

        tG = []
        for hp in range(4):
            b, ft = banks.alloc()
            for hh in range(2):
                h = hp * 2 + hh
                for cs, tab in enumerate((cdt, sdt)):
                    col = (hh * 2 + cs) * 128
                    tk = MM(bank(b)[:, col:col + 128], tab, wfb[:, h, :], True, True,
                            waits=[t_const, t_pc] + ft, sig=(hh == 1 and cs == 1))
            te = ACOPY(Gflat[:, hp * 512:(hp + 1) * 512], bank(b), [tk])
            banks.release(b, te)
            tG.append(te)
        tG_all = tG[-1]

        s_xt = [newsem() for _ in range(2)]
        xt_free = [[], []]
        ht_free = [None, None]
        hT_tok = []
        last_h = None
        for t in range(NT):
            i = t % 2
            tx = DMA(SP, xt[i], x_d[t * 128:(t + 1) * 128, :], s_xt[i], waits=xt_free[i])
            ss = stat[:, t:t + 1]
            sdv = stat[:, 8 + t:9 + t]
            rs = stat[:, 16 + t:17 + t]
            w0 = [tx, t_eps] + ([ht_free[i]] if ht_free[i] else [])
            tss = ACTF(ht[i], xt[i], AF.Square, waits=w0, accum=ss)
            tr = rstd_chain(ss, sdv, rs, D, tss)
            th = VSTT(ht[i], xt[i], rs, gb, ALU.mult, ALU.mult, waits=[tr, tss, t_const])
            last_h = th
            xt_free[i] = [th, tss]
            b, ft = banks.alloc(2)
            pT = bank_bf(b, 2).rearrange("p (c k) -> p c k", c=16)[:, :, 0:128]
            for dc in range(DC):
                tk = TR(pT[:, dc, :], ht[i][:, dc * 128:(dc + 1) * 128],
                        waits=[th, t_const] + ft + ([tG_all] if t == 0 else []), sig=(dc == DC - 1))
            ht_free[i] = tk
            te1 = ACOPY(hT[:, 0:8, t * 128:(t + 1) * 128], pT[:, 0:8, :], [tk])
            te2 = VCOPY(hT[:, 8:16, t * 128:(t + 1) * 128], pT[:, 8:16, :], [tk])
            banks.release(b, te1)
            banks.release(b + 1, te2)
            hT_tok.append([te1, te2])
        s_gb = newsem()
        t_gb_ffn = DMA(SP, gb, gffn_d, s_gb, waits=[last_h])

        def hT_waits(nh):
            w = []
            for t in range(nh * 4, nh * 4 + 4):
                w += hT_tok[t]
            return w

        aT_ev = []
        for m in range(8):
            for nh in range(2):
                b, ft = banks.alloc()
                for dc in range(DC):
                    tk = MM(bank(b), au_buf[m % 4][:, dc, :], hT[:, dc, nh * 512:(nh + 1) * 512],
                            dc == 0, dc == DC - 1,
                            waits=([au_tok[m]] + hT_waits(nh) + ft) if dc == 0 else (), sig=(dc == DC - 1))
                te = evac_copy(aT[:, m, nh * 512:(nh + 1) * 512], bank(b), [tk])
                banks.release(b, te)
                aT_ev.append(te)
            au_free[m] = tk
            if m + 4 < 16:
                load_au(m + 4)
        s_ao = newsem()
        t_ao = DMA(SP, ib.ap().rearrange("(m p) t -> p m t", p=P), aT, s_ao, waits=aT_ev[-2:] + aT_ev[-4:-2])
        POOL.wait(t_ao)
        cc_sem.n += 1
        POOL.ops.append(lambda e: e.collective_compute(
            "AllGather", ALU.bypass, replica_groups=[list(range(NCORES))],
            ins=[ib[:, :]], outs=[ob[:, :]]).then_inc(cc_sem.h, 1))
        t_cc = (cc_sem, 1)

        uT_tok = {}
        for m in range(8):
            i = 8 + m
            for nh in range(2):
                b, ft = banks.alloc()
                for dc in range(DC):
                    tk = MM(bank(b), au_buf[i % 4][:, dc, :], hT[:, dc, nh * 512:(nh + 1) * 512],
                            dc == 0, dc == DC - 1,
                            waits=([au_tok[i]] + ft) if dc == 0 else (), sig=(dc == DC - 1))
                te = ACTF(uT[:, m, nh * 512:(nh + 1) * 512], bank(b), AF.Gelu_apprx_tanh, waits=[tk])
                banks.release(b, te)
                uT_tok[(m, nh)] = te
            au_free[i] = tk
            if i + 4 < 16:
                load_au(i + 4)
        t_pe_u_end = tk

        vfs_free = [None, None]
        vn_tok = []
        for t in range(NT):
            i = t % 2
            tg = []
            for nh in range(2):
                b, ft = banks.alloc()
                for dc in range(DC):
                    tk = MM(bank(b), hT[:, dc, t * 128:(t + 1) * 128], wv[:, dc, nh * 512:(nh + 1) * 512],
                            dc == 0, dc == DC - 1,
                            waits=([t_wv] + ft) if dc == 0 else (), sig=(dc == DC - 1))
                w = [tk] + ([vfs_free[i]] if vfs_free[i] else [])
                te = ACTF(vfs[i][:, nh * 512:(nh + 1) * 512], bank(b), AF.Gelu_apprx_tanh, waits=w)
                banks.release(b, te)
                tg.append(te)
            ss = stat[:, 24 + t:25 + t]
            sdv = stat[:, 32 + t:33 + t]
            rs = stat[:, 40 + t:41 + t]
            tss = ACTF(vn[:, t, :], vfs[i], AF.Square, waits=[tg[1], t_ao], accum=ss)
            tr = rstd_chain(ss, sdv, rs, 1024, tss)
            tv = VTS(vn[:, t, :], vfs[i], rs, None, ALU.mult, waits=[tr, tss, t_ao])
            vfs_free[i] = tv
            vn_tok.append(tv)
        t_pe_v_end = tk

        sp_free = [None, None]
        yg_tok = []
        nsp = 0
        for h in range(8):
            for tgp in range(2):
                b, ft = banks.alloc()
                for tt in range(4):
                    t = tgp * 4 + tt
                    tk = MM(bank(b)[:, tt * 128:(tt + 1) * 128], vn[:, t, h * 128:(h + 1) * 128], wst[:, h, :],
                            True, True, waits=[vn_tok[t], t_pc] + (ft if tt == 0 else []), sig=(tt == 3))
                i = nsp % 2
                nsp += 1
                w = [tk, t_const] + ([sp_free[i]] if sp_free[i] else [])
                bbv = bbt[:, h, :].unsqueeze(1).broadcast_to([P, 4, 128])
                t1 = VSTT(sptmp[i].rearrange("p (a b) -> p a b", a=4), bank(b).rearrange("p (a b) -> p a b", a=4),
                          sgt[:, h:h + 1], bbv, ALU.mult, ALU.add, waits=w)
                banks.release(b, t1)
                t2 = VTT(yT[:, 8 + h, tgp * 512:(tgp + 1) * 512], sptmp[i], uT[:, h, tgp * 512:(tgp + 1) * 512],
                         ALU.mult, waits=[t1, uT_tok[(h, tgp)]])
                sp_free[i] = t2
                yg_tok.append(t2)
        t_pe_sp_end = tk
        t_dve_p3_end = yg_tok[-1]

        s_xr = newsem()
        for t in range(NT):
            t_xr = DMA(SP, x1[:, t, :], x_d[t * 128:(t + 1) * 128, :], s_xr,
                       waits=[t_pe_sp_end, t_dve_p3_end])
        s_wo = [newsem() for _ in range(2)]
        wo_tok = {}
        wo_tok[0] = DMA(POOL, wo_buf[0], wo_d[0], s_wo[0], waits=[t_pe_v_end])

        s_abh = newsem()
        obv = ob.ap().rearrange("(r m p) t -> p m r t", r=NCORES, m=8, p=P)
        abh3 = abh.rearrange("p (r t) -> p r t", r=NCORES)
        pairs = [(b_, h_) for b_ in range(2) for h_ in range(8)]
        abh_free = []
        fold_tok = {}
        bh_tok = {}
        B_free = [[], []]
        yf_tok = {}

        def stage_load_fold(i):
            b_, h_ = pairs[i]
            w = [t_cc, t_pe_sp_end, t_pe_u_end] + abh_free
            tl = DMA(SP, abh3, obv[:, h_, :, b_ * TB:(b_ + 1) * TB], s_abh, waits=w)
            w2 = [tl, t_am0] + ([fold_tok[i - 1][3]] if i > 0 else [])
            ta = VTT(apb[:, 1:2048], abh[:, 1:2048], abh[:, 4095:2048:-1], ALU.add, waits=w2)
            tb = VTT(amb[:, 1:2048], abh[:, 1:2048], abh[:, 4095:2048:-1], ALU.subtract)
            tc = VCOPY(apb[:, 0:1], abh[:, 0:1])
            fold_tok[i] = [ta, tb, tc, None, tl]

        def stage_gmm(i):
            b_, h_ = pairs[i]
            k = i % 2
            ta, tb, tc, _, tl = fold_tok[i]
            b, ft = banks.alloc()
            tk = MM(bank(b)[0:1, 0:128], abh[:, 2048:2049], G[:, h_, 0, :], True, True,
                    waits=[tl, tG_all] + ft, sig=True)
            te = ACOPY(bhalf[k][0:1, :], bank(b)[0:1, 0:128], [tk])
            banks.release(b, te)
            bh_tok[i] = [te, None]
            evs = []
            for (src, cs, dstB) in ((apb, 0, Bp[k]), (amb, 1, Bm[k])):
                for half in range(2):
                    b, ft = banks.alloc(2)
                    for c8 in range(8):
                        c = half * 8 + c8
                        tk = MM(bank(b, 2)[:, c8 * 128:(c8 + 1) * 128], src[:, c * 128:(c + 1) * 128],
                                G[:, h_, cs, :], True, True,
                                waits=([ta, tb, tc] + ft + B_free[k]) if c8 == 0 else (), sig=(c8 == 7))
                    te = evac_copy(dstB[:, half * 8:(half + 1) * 8, :],
                                   bank(b, 2).rearrange("p (a b) -> p a b", a=8), [tk])
                    banks.release(b, te, 2)
                    evs.append(te)
            B_free[k] = []
            fold_tok[i][3] = tk
            abh_free.clear()
            abh_free.extend([tk, ta, tb, tc])
            return evs

        def stage_dft(i, evs):
            b_, h_ = pairs[i]
            k = i % 2
            b, ft = banks.alloc()
            n = 0
            for (srcB, tab) in ((Bp[k], ctab), (Bm[k], nstab)):
                for c in range(DC):
                    MM(bank(b), srcB[:, c, :], tab[:, c, :], n == 0, False,
                       waits=(evs + [t_tab] + ft) if n == 0 else ())
                    n += 1
            tk = MM(bank(b), bhalf[k][0:1, :], sgn[0:1, :], False, True, waits=[bh_tok[i][0], t_const], sig=True)
            bh_tok[i][1] = tk
            B_free[k] = [tk]
            te = ACOPY(yT[:, h_, b_ * TB:(b_ + 1) * TB], bank(b), [tk, t_dve_p3_end])
            banks.release(b, te)
            yf_tok[i] = te

        t_am0 = DVE.op(lambda e: e.memset(amb[:, 0:1], 0.0), waits=[t_pe_u_end, t_pe_sp_end], sig=True)
        stage_load_fold(0)
        evs_prev = stage_gmm(0)
        for i in range(16):
            if i + 1 < 16:
                stage_load_fold(i + 1)
                evs_next = stage_gmm(i + 1)
            stage_dft(i, evs_prev)
            evs_prev = evs_next
        t_pe_dft_end = bh_tok[15][1]
        t_yf_end = yf_tok[15]

        s_gu = [newsem() for _ in range(8)]
        gu_tok = {}
        gu_free = {}
        h2_tok = []
        add_tok = {}
        t_pe_wo_end = None
        h2t_free = [None, None]
        pend_tr = []

        def emit_h2_transposes(t, th):
            i = t % 2
            b, ft = banks.alloc(2)
            pT = bank_bf(b, 2).rearrange("p (c k) -> p c k", c=16)[:, :, 0:128]
            for dc in range(DC):
                tk = TR(pT[:, dc, :], h2t[i][:, dc * 128:(dc + 1) * 128],
                        waits=([th] + ft) if dc == 0 else (), sig=(dc == DC - 1))
            h2t_free[i] = tk
            te1 = ACOPY(h2T[:, 0:8, t * 128:(t + 1) * 128], pT[:, 0:8, :], [tk, t_pe_dft_end])
            te2 = VCOPY(h2T[:, 8:16, t * 128:(t + 1) * 128], pT[:, 8:16, :], [tk, t_pe_dft_end])
            banks.release(b, te1)
            banks.release(b + 1, te2)
            h2_tok.append([te1, te2])

        wo_free_tok = {}
        for j in range(4):
            if j + 1 < 4:
                if j + 1 == 1:
                    w = [t_pe_dft_end]
                else:
                    w = [wo_free_tok[(j + 1) % 2]]
                wo_tok[j + 1] = DMA(POOL, wo_buf[(j + 1) % 2], wo_d[j + 1], s_wo[(j + 1) % 2], waits=w)
            for t in range(NT):
                b, ft = banks.alloc()
                for c in range(DC):
                    tk = MM(bank(b), yT[:, c, t * 128:(t + 1) * 128], wo_buf[j % 2][:, c, :],
                            c == 0, c == DC - 1,
                            waits=([wo_tok[j], t_yf_end, t_dve_p3_end] + ft) if c == 0 else (), sig=(c == DC - 1))
                sl = slice(j * 512, (j + 1) * 512)
                ta = VTT(x1[:, t, sl], bank(b), x1[:, t, sl], ALU.add, waits=[tk, t_xr])
                banks.release(b, ta)
                add_tok[(t, j)] = ta
                if j == 3:
                    i = t % 2
                    ss = stat[:, t:t + 1]
                    sdv = stat[:, 8 + t:9 + t]
                    rs = stat[:, 16 + t:17 + t]
                    w0 = [ta] + ([h2t_free[i]] if h2t_free[i] else []) + [t_pe_dft_end]
                    tss = ACTF(h2t[i], x1[:, t, :], AF.Square, waits=w0, accum=ss)
                    tr = rstd_chain(ss, sdv, rs, D, tss)
                    th = VSTT(h2t[i], x1[:, t, :], rs, gb, ALU.mult, ALU.mult, waits=[tr, tss, t_gb_ffn])
                    pend_tr.append((t, th))
                    if len(pend_tr) > 1:
                        emit_h2_transposes(*pend_tr.pop(0))
            wo_free_tok[j % 2] = tk
        t_pe_wo_end = tk
        while pend_tr:
            emit_h2_transposes(*pend_tr.pop(0))
        t_h2_dve_end = th
        t_gb_fin = DMA(SP, gb, gfin_d, s_gb, waits=[t_h2_dve_end])

        def load_gu(fc):
            w = []
            if fc >= 4:
                w = [gu_free[fc - 4]]
            else:
                w = [t_pe_wo_end, h2t_free[0], h2t_free[1]]
            s0 = (fc % 4) * 2
            DMA(POOL, gu_buf[s0], wg_d[fc], s_gu[s0], waits=w)
            gu_tok[fc] = [(s_gu[s0], s_gu[s0].n), DMA(POOL, gu_buf[s0 + 1], wup_d[fc], s_gu[s0 + 1])]

        s_wd = [newsem() for _ in range(2)]
        wd_tok = {}
        wd_free = {}
        nwd = [0]

        def load_wd(q, j):
            idx = q * 4 + j
            w = []
            if idx >= 2:
                w = [wd_free[idx - 2]]
            else:
                w = [t_pe_wo_end]
            wd_tok[idx] = DMA(POOL, wd_buf[idx % 2], wd_d[q, j], s_wd[idx % 2], waits=w)

        for fc in range(4):
            load_gu(fc)
        load_wd(0, 0)
        load_wd(0, 1)

        def h2_waits(nh):
            w = []
            for t in range(nh * 4, nh * 4 + 4):
                w += h2_tok[t]
            return w

        sg_free = [None, None]
        nsg = 0
        hid_free = {}
        s_out = newsem()
        out_tok = None
        for q in range(NQ):
            hid_tok = {}
            for f in range(FQ):
                fc = q * FQ + f
                for nh in range(2):
                    bg, ftg = banks.alloc()
                    for dc in range(DC):
                        tkg = MM(bank(bg), gu_buf[(fc % 4) * 2][:, dc, :], h2T[:, dc, nh * 512:(nh + 1) * 512],
                                 dc == 0, dc == DC - 1,
                                 waits=([gu_tok[fc][0]] + (h2_waits(nh) if q == 0 else []) + ftg) if dc == 0 else (),
                                 sig=(dc == DC - 1))
                    bu, ftu = banks.alloc()
                    for dc in range(DC):
                        tku = MM(bank(bu), gu_buf[(fc % 4) * 2 + 1][:, dc, :], h2T[:, dc, nh * 512:(nh + 1) * 512],
                                 dc == 0, dc == DC - 1,
                                 waits=([gu_tok[fc][1]] + ftu) if dc == 0 else (), sig=(dc == DC - 1))
                    i = nsg % 2
                    nsg += 1
                    w = [tkg] + ([sg_free[i]] if sg_free[i] else [])
                    ts = ACTF(sgtmp[i], bank(bg), AF.Silu, waits=w)
                    banks.release(bg, ts)
                    w = [ts, tku] + ([hid_free[f]] if f in hid_free else [])
                    tm = VTT(hid(f)[:, nh * 512:(nh + 1) * 512], bank(bu), sgtmp[i], ALU.mult, waits=w)
                    banks.release(bu, tm)
                    sg_free[i] = tm
                    hid_tok[(f, nh)] = tm
                gu_free[fc] = tku
                if fc + 4 < NFC:
                    load_gu(fc + 4)
            for j in range(4):
                idx = q * 4 + j
                for t in range(NT):
                    b, ft = banks.alloc()
                    nh = t // 4
                    for f in range(FQ):
                        tk = MM(bank(b), hid(f)[:, t * 128:(t + 1) * 128], wd_buf[idx % 2][:, f, :],
                                f == 0, f == FQ - 1,
                                waits=([hid_tok[(f, nh)]] + ([wd_tok[idx]] + ft if f == 0 else [])),
                                sig=(f == FQ - 1))
                    sl = slice(j * 512, (j + 1) * 512)
                    ta = VTT(x1[:, t, sl], bank(b), x1[:, t, sl], ALU.add, waits=[tk, add_tok[(t, 3)]])
                    banks.release(b, ta)
                    if q == NQ - 1 and j == 3:
                        ss = stat[:, t:t + 1]
                        sdv = stat[:, 8 + t:9 + t]
                        rs = stat[:, 16 + t:17 + t]
                        tss = ACTF(junkC, x1[:, t, :], AF.Square, waits=[ta, tku], accum=ss)
                        tr = rstd_chain(ss, sdv, rs, D, tss)
                        tf = VSTT(x1[:, t, :], x1[:, t, :], rs, gb, ALU.mult, ALU.mult, waits=[tr, tss, t_gb_fin])
                        out_tok = DMA(SP, out_d[t * 128:(t + 1) * 128, :], x1[:, t, :], s_out, waits=[tf])
                wd_free[idx] = tk
                if idx + 2 < NQ * 4:
                    load_wd((idx + 2) // 4, (idx + 2) % 4)
            for f in range(FQ):
                hid_free[f] = tk
        SP.wait(out_tok)

        @block.tensor
        def _(e):
            PE.run(e)

        @block.scalar
        def _(e):
            ACT.run(e)

        @block.vector
        def _(e):
            DVE.run(e)

        @block.gpsimd
        def _(e):
            POOL.run(e)

        @block.sync
        def _(e):
            SP.run(e)

    return nc


_CACHE = {}


def _tables(c):
    kg = (512 * c + np.arange(512)).astype(np.int64)
    n = np.arange(2048, dtype=np.int64)
    ph = (n[:, None] * kg[None, :]) % SEQ
    ang = 2.0 * np.pi * ph.astype(np.float64) / SEQ
    sc = 1.0 / np.sqrt(SEQ)
    ct = (np.cos(ang) * sc).reshape(DC, P, 512).transpose(1, 0, 2)
    st = (-np.sin(ang) * sc).reshape(DC, P, 512).transpose(1, 0, 2)
    sg = (np.where(kg % 2 == 0, 1.0, -1.0) * sc).reshape(1, 512)
    bf = ml_dtypes.bfloat16
    return (np.ascontiguousarray(ct).astype(bf), np.ascontiguousarray(st).astype(bf), sg.astype(bf))


def kernel(x, norm_mix, w_in, w_fourier, sgu_norm, w_spatial, b_spatial, w_out,
           norm_ffn, w_gate, w_up, w_down, norm_final):
    f32 = np.float32
    bf = ml_dtypes.bfloat16
    x = np.asarray(x, f32)
    w_in = np.asarray(w_in, f32)[0]
    w_out = np.asarray(w_out, f32)[0]
    w_gate = np.asarray(w_gate, f32)[0]
    w_up = np.asarray(w_up, f32)[0]
    w_down = np.asarray(w_down, f32)[0]

    def lhs_tiles(w, ncol_chunks):
        return np.ascontiguousarray(w.reshape(DC, P, ncol_chunks, 128).transpose(2, 1, 0, 3))

    shared = {}
    shared["wa"] = lhs_tiles(w_in[:, 0:1024], 8)
    shared["wu"] = lhs_tiles(w_in[:, 1024:2048], 8)
    shared["wv"] = np.ascontiguousarray(w_in[:, 2048:3072].reshape(DC, P, 1024).transpose(1, 0, 2))
    shared["wo"] = np.ascontiguousarray(w_out.reshape(DC, P, 4, 512).transpose(2, 1, 0, 3))
    shared["wg"] = lhs_tiles(w_gate, NFC)
    shared["wup"] = lhs_tiles(w_up, NFC)
    shared["wd"] = np.ascontiguousarray(w_down.reshape(NQ, FQ, P, 4, 512).transpose(0, 3, 2, 1, 4))
    shared["wf"] = np.ascontiguousarray(np.asarray(w_fourier, f32)[0].transpose(1, 0, 2))
    shared["wst"] = np.ascontiguousarray(np.asarray(w_spatial, f32)[0].transpose(2, 0, 1))
    shared["bb"] = np.ascontiguousarray(np.broadcast_to(np.asarray(b_spatial, f32)[0][None], (P, 8, 128)))
    shared["sg"] = np.ascontiguousarray(np.asarray(sgu_norm, f32)[0].reshape(8, P).T)
    shared["gmix"] = np.ascontiguousarray(np.broadcast_to(np.asarray(norm_mix, f32)[0][None], (P, D)))
    shared["gffn"] = np.ascontiguousarray(np.broadcast_to(np.asarray(norm_ffn, f32)[0][None], (P, D)))
    shared["gfin"] = np.ascontiguousarray(np.broadcast_to(np.asarray(norm_final, f32)[None], (P, D)))
    dd = np.arange(128, dtype=np.int64)
    angd = 2.0 * np.pi * ((dd[:, None] * dd[None, :]) % 128).astype(np.float64) / 128
    shared["cd"] = (np.cos(angd) / np.sqrt(128.0)).astype(bf)
    shared["sd"] = (np.sin(angd) / np.sqrt(128.0)).astype(bf)
    shared["ident"] = np.eye(128).astype(bf)

    in_maps = []
    for c in range(NCORES):
        m = dict(shared)
        m["x"] = np.ascontiguousarray(np.concatenate([x[0, 512 * c:512 * c + 512], x[1, 512 * c:512 * c + 512]], 0))
        ct, st, sg = _tables(c)
        m["ctab"], m["nstab"], m["sgn"] = ct, st, sg
        in_maps.append(m)

    if "nc" not in _CACHE:
        _CACHE["nc"] = build_nc()
    nc = _CACHE["nc"]
    res = run_bass_kernel_spmd(nc, in_maps, core_ids=list(range(NCORES)))
    out = np.empty((2, SEQ, D), f32)
    for c in range(NCORES):
        o = np.asarray(res.results[c]["out"], f32)
        out[0, 512 * c:512 * c + 512] = o[0:512]
        out[1, 512 * c:512 * c + 512] = o[512:1024]
    return out
```

```python
import numpy as np
import ml_dtypes
import concourse.bass as bass
import concourse.mybir as mybir
from concourse.bass_utils import run_bass_kernel_spmd

F32 = mybir.dt.float32
BF16 = mybir.dt.bfloat16
AF = mybir.ActivationFunctionType
ALU = mybir.AluOpType

NCORES = 8
P = 128
D = 2048
DC = 16
SEQ = 4096
TOK = 1024
NT = 8
TB = 512
DFF = 5632
NFC = 44
NQ = 4
FQ = 11
EPS = 1e-6
KB = 1024


class Sem:
    def __init__(self, h):
        self.h = h
        self.n = 0


class EngQ:
    def __init__(self, name):
        self.name = name
        self.ops = []
        self.sem = None
        self.waited = {}

    def wait(self, tok):
        if tok is None:
            return
        sem, val = tok
        if self.waited.get(id(sem), 0) >= val:
            return
        self.waited[id(sem)] = val
        self.ops.append(lambda e: e.wait_ge(sem.h, val))

    def op(self, fn, waits=(), sig=False, inc=None):
        for w in waits:
            self.wait(w)
        if sig:
            sem = self.sem
            sem.n += 1
            v = sem.n
            self.ops.append(lambda e: fn(e).then_inc(sem.h, 1))
            return (sem, v)
        if inc is not None:
            inc.n += 16
            v = inc.n
            self.ops.append(lambda e: fn(e).then_inc(inc.h, 16))
            return (inc, v)
        self.ops.append(fn)
        return None

    def run(self, e):
        for f in self.ops:
            f(e)


class Banks:
    def __init__(self):
        self.next = 0
        self.free = [[] for _ in range(8)]

    def alloc(self, n=1):
        if n == 2 and self.next % 2:
            self.next = (self.next + 1) % 8
        b = self.next
        self.next = (self.next + n) % 8
        toks = []
        for i in range(n):
            toks += self.free[b + i]
            self.free[b + i] = []
        return b, toks

    def release(self, b, tok, n=1):
        for i in range(n):
            self.free[b + i].append(tok)


def build_nc():
    nc = bass.Bass("TRN2", target_bir_lowering=False)

    def din(name, shape, dt=F32):
        return nc.dram_tensor(name, list(shape), dt, kind="ExternalInput").ap()

    x_d = din("x", [TOK, D])
    gmix_d = din("gmix", [P, D + 8])
    gffn_d = din("gffn", [P, D + 8])
    gfin_d = din("gfin", [P, D + 8])
    wa_d = din("wa", [8, P, DC + 1, 128])
    wu_d = din("wu", [8, P, DC + 1, 128])
    wv_d = din("wv", [P, DC + 1, 1024])
    wo_d = din("wo", [4, P, DC + 1, 512])
    wg_d = din("wg", [NFC, P, DC + 1, 128])
    wup_d = din("wup", [NFC, P, DC + 1, 128])
    wd_d = din("wd", [NQ, 4, P, FQ + 1, 512])
    wf_d = din("wf", [P, 8, 128])
    wst_d = din("wst", [P, 8, 128])
    bb_d = din("bb", [P, 8, 128])
    sg_d = din("sg", [P, 8])
    ctab_d = din("ctab", [P, DC, 512], BF16)
    nstab_d = din("nstab", [P, DC, 512], BF16)
    sgn_d = din("sgn", [1, 512], BF16)
    cd_d = din("cd", [P, 128], BF16)
    sd_d = din("sd", [P, 128], BF16)
    id_d = din("ident", [P, 128], BF16)
    out_d = nc.dram_tensor("out", [TOK, D], F32, kind="ExternalOutput").ap()

    ARENA = 200 * KB
    n_sems = 52
    from contextlib import ExitStack
    with ExitStack() as es:
        arena = es.enter_context(nc.sbuf_tensor("arena", [P, ARENA // 2], BF16))
        psum = es.enter_context(nc.psum_tensor("psum", [P, 4096], F32))
        sems = [Sem(es.enter_context(nc.semaphore(f"s{i}"))) for i in range(n_sems)]
        block = es.enter_context(nc.Block())
        sem_it = iter(sems)

        def newsem():
            return next(sem_it)

        PE, ACT, DVE, POOL, SP = EngQ("pe"), EngQ("act"), EngQ("dve"), EngQ("pool"), EngQ("sp")
        PE.sem, ACT.sem, DVE.sem = newsem(), newsem(), newsem()
        cc_sem = newsem()

        def vb(off, shape):
            n = int(np.prod(shape))
            a = arena[:, off // 2: off // 2 + n]
            if len(shape) == 2:
                a = a.rearrange("p (a b) -> p a b", a=shape[0])
            elif len(shape) == 3:
                a = a.rearrange("p (a b c) -> p a b c", a=shape[0], b=shape[1])
            return a

        def vf(off, shape):
            n = int(np.prod(shape))
            a = arena[:, off // 2: off // 2 + 2 * n].bitcast(F32)
            if len(shape) == 2:
                a = a.rearrange("p (a b) -> p a b", a=shape[0])
            return a

        def bank(b, n=1):
            return psum[:, b * 512:(b + n) * 512]

        def bank_bf(b, n=1):
            return psum[:, b * 512:(b + n) * 512].bitcast(BF16)

        hT = vb(0, [DC, TOK])
        uT = vb(32 * KB, [8, TOK])
        aT = vb(48 * KB, [8, TOK])
        vn = vb(48 * KB, [NT, 1024])
        x1 = vf(0, [NT, D])
        wfb = vb(32 * KB, [8, 128])
        xt = [vf(144 * KB + i * 8 * KB, [D]) for i in range(4)] + [vf(64 * KB + i * 8 * KB, [D]) for i in range(2)]
        ht = [vb(80 * KB + i * 4 * KB, [D]) for i in range(4)]
        vfs = [vf(64 * KB + i * 4 * KB, [1024]) for i in range(2)]
        sptmp = [vf(72 * KB + i * 2 * KB, [512]) for i in range(2)]
        yT = vb(64 * KB, [16, TOK])
        wd_buf = [vb(64 * KB + i * 11 * KB, [FQ, 512]) for i in range(2)]
        hidA = vb(86 * KB, [5, TOK])
        ctab = vb(96 * KB, [DC, 512])
        nstab = vb(112 * KB, [DC, 512])
        h2T = vb(96 * KB, [DC, TOK])
        junkC = vb(96 * KB, [D])
        au_buf = [vb(128 * KB + i * 4 * KB, [DC, 128]) for i in range(4)]
        wv = vb(144 * KB, [DC, 1024])
        abh = vb(128 * KB, [4096])
        apb = vb(136 * KB, [2048])
        amb = vb(140 * KB, [2048])
        Bp = [vb(144 * KB + i * 8 * KB, [DC, 128]) for i in range(2)]
        Bm = [vb(148 * KB + i * 8 * KB, [DC, 128]) for i in range(2)]
        wo_buf = [vb(160 * KB, [DC, 512]), vb(128 * KB, [DC, 512])]
        h2t = [vb(144 * KB + i * 4 * KB, [D]) for i in range(2)]
        gu_buf = [vb(128 * KB + i * 4 * KB, [DC, 128]) for i in range(8)]
        hidB = vb(160 * KB, [6, TOK])
        sgtmp = [vf(172 * KB + i * 2 * KB, [512]) for i in range(2)]
        gb = vf(176 * KB, [D])
        o = 184 * KB
        ident = vb(o, [128]); o += 256
        cdt = vb(o, [128]); o += 256
        sdt = vb(o, [128]); o += 256
        Gflat = vb(o, [2048]); G = vb(o, [8, 2, 128]); o += 4096
        wst = vb(o, [8, 128]); o += 2048
        bbt = vf(o, [8, 128]); o += 4096
        sgt = vf(o, [8]); o += 32
        sgn = vb(o, [512]); o += 1024
        bhalf = [vb(o + i * 256, [128]) for i in range(2)]; o += 512
        stat = vf(o, [64]); o += 256
        epsb = vf(o, [1]); o += 32
        assert o <= ARENA

        def hid(f):
            return hidA[:, f, :] if f < 5 else hidB[:, f - 5, :]

        banks = Banks()

        def MM(out, lhsT, rhs, start, stop, waits=(), sig=False):
            return PE.op(lambda e: e.matmul(out, lhsT, rhs, start=start, stop=stop), waits, sig)

        def TR(out, in_, waits=(), sig=False):
            return PE.op(lambda e: e.transpose(out, in_, ident), waits, sig)

        def ACTF(out, in_, func, waits=(), sig=True, bias=None, scale=None, accum=None):
            kw = {}
            if bias is not None:
                kw["bias"] = bias
            if scale is not None:
                kw["scale"] = scale
            if accum is not None:
                kw["accum_out"] = accum
            return ACT.op(lambda e: e.activation(out, in_, func, **kw), waits, sig)

        def ACOPY(out, in_, waits=(), sig=True):
            return ACT.op(lambda e: e.copy(out, in_), waits, sig)

        def VCOPY(out, in_, waits=(), sig=True):
            return DVE.op(lambda e: e.tensor_copy(out, in_), waits, sig)

        def VTT(out, in0, in1, op, waits=(), sig=True):
            return DVE.op(lambda e: e.tensor_tensor(out, in0, in1, op), waits, sig)

        def VTS(out, in0, s1, s2, op0, op1=None, waits=(), sig=True):
            if op1 is None:
                return DVE.op(lambda e: e.tensor_scalar(out, in0, s1, None, op0), waits, sig)
            return DVE.op(lambda e: e.tensor_scalar(out, in0, s1, s2, op0, op1), waits, sig)

        def VSTT(out, in0, scalar, in1, op0, op1, waits=(), sig=True):
            return DVE.op(lambda e: e.scalar_tensor_tensor(out, in0, scalar, in1, op0, op1), waits, sig)

        def VRECIP(out, in_, waits=(), sig=True):
            return DVE.op(lambda e: e.reciprocal(out, in_), waits, sig)

        def DMA(q, out, in_, sem, waits=()):
            return q.op(lambda e: e.dma_start(out=out, in_=in_), waits, inc=sem)

        def rstd_chain(ss_ap, sd_ap, r_ap, n, tok_ss):
            t1 = ACTF(sd_ap, ss_ap, AF.Sqrt, waits=[tok_ss], bias=epsb[:, 0:1], scale=1.0 / n)
            return VRECIP(r_ap, sd_ap, waits=[t1])

        evac_flip = [0]

        def evac_copy(out, in_, waits):
            evac_flip[0] ^= 1
            if evac_flip[0]:
                return ACOPY(out, in_, waits)
            return VCOPY(out, in_, waits)

        s_const = newsem()
        for dst, src in ((ident, id_d), (gb, gmix_d[:, 0:D]), (cdt, cd_d), (sdt, sd_d), (bbt, bb_d), (sgt, sg_d)):
            t_const = DMA(SP, dst, src, s_const)
        t_const = DMA(SP, sgn[0:1, :], sgn_d, s_const)
        s_pc = newsem()
        DMA(POOL, wfb, wf_d, s_pc)
        t_pc = DMA(POOL, wst, wst_d, s_pc)
        t_eps = DVE.op(lambda e: e.memset(epsb, EPS), sig=True)

        s_au = [newsem() for _ in range(4)]
        au_tok = {}
        au_free = {}
        def load_au(i):
            src = wa_d[i % 8] if i < 16 else wu_d[i - 16]
            waits = [au_free[i - 4]] if i >= 4 else []
            au_tok[i] = DMA(POOL, au_buf[i % 4], src[:, 0:DC, :], s_au[i % 4], waits)
        for i in range(4):
            load_au(i)

        tG = []
        for hp in range(4):
            b, ft = banks.alloc()
            for hh in range(2):
                h = hp * 2 + hh
                for cs, tab in enumerate((cdt, sdt)):
                    col = (hh * 2 + cs) * 128
                    tk = MM(bank(b)[:, col:col + 128], tab, wfb[:, h, :], True, True,
                            waits=[t_const, t_pc] + ft, sig=(hh == 1 and cs == 1))
            te = ACOPY(Gflat[:, hp * 512:(hp + 1) * 512], bank(b), [tk])
            banks.release(b, te)
            tG.append(te)
        tG_all = tG[-1]

        NX, NH = 6, 4
        s_xt = [newsem() for _ in range(NX)]
        xt_free = [[] for _ in range(NX)]
        ht_free = [None] * NH
        h_tok = {}
        tss_tok = {}
        hT_tok = {}

        def norm_part(t):
            i, k = t % NX, t % NH
            tx = DMA(SP, xt[i], x_d[t * 128:(t + 1) * 128, :], s_xt[i], waits=xt_free[i])
            ss = stat[:, t:t + 1]
            sdv = stat[:, 8 + t:9 + t]
            rs = stat[:, 16 + t:17 + t]
            w0 = [tx, t_eps] + ([ht_free[k]] if ht_free[k] else [])
            tss = ACTF(ht[k], xt[i], AF.Square, waits=w0, accum=ss)
            tr = rstd_chain(ss, sdv, rs, D, tss)
            th = VSTT(ht[k], xt[i], rs, gb, ALU.mult, ALU.mult, waits=[tr, tss, t_const])
            xt_free[i] = [th, tss]
            h_tok[t] = th
            tss_tok[t] = tss

        def tr_part(t):
            k = t % NH
            b, ft = banks.alloc(2)
            pT = bank_bf(b, 2).rearrange("p (c k) -> p c k", c=16)
            for dc in range(DC):
                tk = TR(pT[:, dc, :], ht[k][:, dc * 128:(dc + 1) * 128],
                        waits=[h_tok[t], t_const] + ft, sig=(dc == DC - 1))
            ht_free[k] = tk
            te1 = ACOPY(hT[:, 0:8, t * 128:(t + 1) * 128], pT[:, 0:8, :], [tk, tG_all])
            te2 = VCOPY(hT[:, 8:16, t * 128:(t + 1) * 128], pT[:, 8:16, :], [tk, tG_all])
            banks.release(b, te1)
            banks.release(b + 1, te2)
            hT_tok[t] = [te1, te2]

        def hT_waits(nh):
            w = []
            for t in range(nh * 4, nh * 4 + 4):
                w += hT_tok[t]
            return w

        s_ao = [newsem() for _ in range(2)]
        cc_sems = [cc_sem, newsem()]
        ibs = [nc.dram_tensor(f"ib{k}", [1024, TB], BF16) for k in range(2)]
        obs = [nc.dram_tensor(f"ob{k}", [NCORES * 1024, TB], BF16) for k in range(2)]
        t_ao = [None, None]
        t_cc = [None, None]

        def aT_part(nh):
            evs = []
            for m in range(8):
                i = nh * 8 + m
                b, ft = banks.alloc()
                for dc in range(DC):
                    tk = MM(bank(b), au_buf[i % 4][:, dc, :], hT[:, dc, nh * 512:(nh + 1) * 512],
                            dc == 0, dc == DC - 1,
                            waits=([au_tok[i]] + hT_waits(nh) + ft) if dc == 0 else (), sig=(dc == DC - 1))
                te = evac_copy(aT[:, m, nh * 512:(nh + 1) * 512], bank(b), [tk])
                banks.release(b, te)
                evs.append(te)
                au_free[i] = tk
                if i + 4 < 24:
                    load_au(i + 4)
            t_ao[nh] = DMA(SP, ibs[nh].ap().rearrange("(m p) t -> p m t", p=P), aT[:, :, nh * 512:(nh + 1) * 512],
                           s_ao[nh], waits=evs[-2:])
            POOL.wait(t_ao[nh])
            ccs = cc_sems[nh]
            ccs.n += 1
            ibk, obk = ibs[nh], obs[nh]
            POOL.ops.append(lambda e: e.collective_compute(
                "AllGather", ALU.bypass, replica_groups=[list(range(NCORES))],
                ins=[ibk[:, :]], outs=[obk[:, :]]).then_inc(ccs.h, 1))
            t_cc[nh] = (ccs, 1)

        for t in range(4):
            norm_part(t)
            tr_part(t)
        for t in range(4, NT):
            norm_part(t)
        s_tab = newsem()
        DMA(SP, ctab, ctab_d, s_tab)
        t_tab = DMA(SP, nstab, nstab_d, s_tab)
        s_gb = newsem()
        t_gb_ffn = DMA(SP, gb, gffn_d[:, 0:D], s_gb, waits=[h_tok[NT - 1]])
        aT_part(0)
        for t in range(4, NT):
            tr_part(t)
        aT_part(1)
        s_wv = newsem()
        for kq in range(4):
            t_wv = DMA(POOL, wv[:, kq * 4:(kq + 1) * 4, :], wv_d[:, kq * 4:(kq + 1) * 4, :], s_wv,
                       waits=[h_tok[NT - 1], tss_tok[NT - 1]])

        uT_tok = {}
        for m in range(8):
            i = 16 + m
            for nh in range(2):
                b, ft = banks.alloc()
                for dc in range(DC):
                    tk = MM(bank(b), au_buf[i % 4][:, dc, :], hT[:, dc, nh * 512:(nh + 1) * 512],
                            dc == 0, dc == DC - 1,
                            waits=([au_tok[i]] + ft) if dc == 0 else (), sig=(dc == DC - 1))
                te = ACTF(uT[:, m, nh * 512:(nh + 1) * 512], bank(b), AF.Gelu_apprx_tanh, waits=[tk])
                banks.release(b, te)
                uT_tok[(m, nh)] = te
            au_free[i] = tk
            if i + 4 < 24:
                load_au(i + 4)
        t_pe_u_end = tk

        vfs_free = [None, None]
        vn_tok = []
        for t in range(NT):
            i = t % 2
            tg = []
            for nh in range(2):
                b, ft = banks.alloc()
                for dc in range(DC):
                    tk = MM(bank(b), hT[:, dc, t * 128:(t + 1) * 128], wv[:, dc, nh * 512:(nh + 1) * 512],
                            dc == 0, dc == DC - 1,
                            waits=([t_wv] + ft) if dc == 0 else (), sig=(dc == DC - 1))
                w = [tk] + ([vfs_free[i]] if vfs_free[i] else [])
                te = ACTF(vfs[i][:, nh * 512:(nh + 1) * 512], bank(b), AF.Gelu_apprx_tanh, waits=w)
                banks.release(b, te)
                tg.append(te)
            ss = stat[:, 24 + t:25 + t]
            sdv = stat[:, 32 + t:33 + t]
            rs = stat[:, 40 + t:41 + t]
            tss = ACTF(vn[:, t, :], vfs[i], AF.Square, waits=[tg[1]] + t_ao, accum=ss)
            tr = rstd_chain(ss, sdv, rs, 1024, tss)
            tv = VTS(vn[:, t, :], vfs[i], rs, None, ALU.mult, waits=[tr, tss] + t_ao)
            vfs_free[i] = tv
            vn_tok.append(tv)
        t_pe_v_end = tk

        sp_free = [None, None]
        yg_tok = []
        nsp = 0
        for h in range(8):
            for tgp in range(2):
                b, ft = banks.alloc()
                for tt in range(4):
                    t = tgp * 4 + tt
                    tk = MM(bank(b)[:, tt * 128:(tt + 1) * 128], vn[:, t, h * 128:(h + 1) * 128], wst[:, h, :],
                            True, True, waits=[vn_tok[t], t_pc] + (ft if tt == 0 else []), sig=(tt == 3))
                i = nsp % 2
                nsp += 1
                w = [tk, t_const] + ([sp_free[i]] if sp_free[i] else [])
                bbv = bbt[:, h, :].unsqueeze(1).broadcast_to([P, 4, 128])
                t1 = VSTT(sptmp[i].rearrange("p (a b) -> p a b", a=4), bank(b).rearrange("p (a b) -> p a b", a=4),
                          sgt[:, h:h + 1], bbv, ALU.mult, ALU.add, waits=w)
                banks.release(b, t1)
                t2 = VTT(yT[:, 8 + h, tgp * 512:(tgp + 1) * 512], sptmp[i], uT[:, h, tgp * 512:(tgp + 1) * 512],
                         ALU.mult, waits=[t1, uT_tok[(h, tgp)]])
                sp_free[i] = t2
                yg_tok.append(t2)
        t_pe_sp_end = tk
        t_dve_p3_end = yg_tok[-1]

        s_xr = newsem()
        for t in range(NT):
            t_xr = DMA(SP, x1[:, t, :], x_d[t * 128:(t + 1) * 128, :], s_xr,
                       waits=[t_pe_sp_end, t_dve_p3_end])
        s_wo = [newsem() for _ in range(2)]
        wo_tok = {}
        wo_tok[0] = DMA(POOL, wo_buf[0], wo_d[0][:, 0:DC, :], s_wo[0], waits=[t_pe_v_end])

        s_abh = newsem()
        obv = [obs[k].ap().rearrange("(r m p) t -> p m r t", r=NCORES, m=8, p=P) for k in range(2)]
        abh3 = abh.rearrange("p (r t) -> p r t", r=NCORES)
        pairs = [(b_, h_) for b_ in range(2) for h_ in range(8)]
        abh_free = []
        fold_tok = {}
        bh_tok = {}
        B_free = [[], []]
        yf_tok = {}

        def stage_load_fold(i):
            b_, h_ = pairs[i]
            w = [t_cc[b_], t_pe_sp_end, t_pe_u_end] + abh_free
            tl = DMA(SP, abh3, obv[b_][:, h_, :, :], s_abh, waits=w)
            w2 = [tl, t_am0] + ([fold_tok[i - 1][3]] if i > 0 else [])
            ta = VTT(apb[:, 1:2048], abh[:, 1:2048], abh[:, 4095:2048:-1], ALU.add, waits=w2)
            tb = VTT(amb[:, 1:2048], abh[:, 1:2048], abh[:, 4095:2048:-1], ALU.subtract)
            tc = VCOPY(apb[:, 0:1], abh[:, 0:1])
            fold_tok[i] = [ta, tb, tc, None, tl]

        def stage_gmm(i):
            b_, h_ = pairs[i]
            k = i % 2
            ta, tb, tc, _, tl = fold_tok[i]
            b, ft = banks.alloc()
            tk = MM(bank(b)[0:1, 0:128], abh[:, 2048:2049], G[:, h_, 0, :], True, True,
                    waits=[tl, tG_all] + ft, sig=True)
            te = ACOPY(bhalf[k][0:1, :], bank(b)[0:1, 0:128], [tk])
            banks.release(b, te)
            bh_tok[i] = [te, None]
            evs = []
            for (src, cs, dstB) in ((apb, 0, Bp[k]), (amb, 1, Bm[k])):
                for half in range(2):
                    b, ft = banks.alloc(2)
                    for c8 in range(8):
                        c = half * 8 + c8
                        tk = MM(bank(b, 2)[:, c8 * 128:(c8 + 1) * 128], src[:, c * 128:(c + 1) * 128],
                                G[:, h_, cs, :], True, True,
                                waits=([ta, tb, tc] + ft + B_free[k]) if c8 == 0 else (), sig=(c8 == 7))
                    te = evac_copy(dstB[:, half * 8:(half + 1) * 8, :],
                                   bank(b, 2).rearrange("p (a b) -> p a b", a=8), [tk])
                    banks.release(b, te, 2)
                    evs.append(te)
            B_free[k] = []
            fold_tok[i][3] = tk
            abh_free.clear()
            abh_free.extend([tk, ta, tb, tc])
            return evs

        def stage_dft(i, evs):
            b_, h_ = pairs[i]
            k = i % 2
            b, ft = banks.alloc()
            n = 0
            for (srcB, tab) in ((Bp[k], ctab), (Bm[k], nstab)):
                for c in range(DC):
                    MM(bank(b), srcB[:, c, :], tab[:, c, :], n == 0, False,
                       waits=(evs + [t_tab] + ft) if n == 0 else ())
                    n += 1
            tk = MM(bank(b), bhalf[k][0:1, :], sgn[0:1, :], False, True, waits=[bh_tok[i][0], t_const], sig=True)
            bh_tok[i][1] = tk
            B_free[k] = [tk]
            te = ACOPY(yT[:, h_, b_ * TB:(b_ + 1) * TB], bank(b), [tk, t_dve_p3_end])
            banks.release(b, te)
            yf_tok[i] = te

        t_am0 = DVE.op(lambda e: e.memset(amb[:, 0:1], 0.0), waits=[t_pe_u_end, t_pe_sp_end], sig=True)
        stage_load_fold(0)
        evs_prev = stage_gmm(0)
        for i in range(16):
            if i + 1 < 16:
                stage_load_fold(i + 1)
                evs_next = stage_gmm(i + 1)
            stage_dft(i, evs_prev)
            evs_prev = evs_next
        t_pe_dft_end = bh_tok[15][1]
        t_yf_end = yf_tok[15]

        s_gu = [newsem() for _ in range(8)]
        gu_tok = {}
        gu_free = {}
        h2_tok = []
        add_tok = {}
        t_pe_wo_end = None
        h2t_free = [None, None]
        pend_tr = []

        def emit_h2_transposes(t, th):
            i = t % 2
            b, ft = banks.alloc(2)
            pT = bank_bf(b, 2).rearrange("p (c k) -> p c k", c=16)[:, :, 0:128]
            for dc in range(DC):
                tk = TR(pT[:, dc, :], h2t[i][:, dc * 128:(dc + 1) * 128],
                        waits=([th] + ft) if dc == 0 else (), sig=(dc == DC - 1))
            h2t_free[i] = tk
            te1 = ACOPY(h2T[:, 0:8, t * 128:(t + 1) * 128], pT[:, 0:8, :], [tk, t_pe_dft_end])
            te2 = VCOPY(h2T[:, 8:16, t * 128:(t + 1) * 128], pT[:, 8:16, :], [tk, t_pe_dft_end])
            banks.release(b, te1)
            banks.release(b + 1, te2)
            h2_tok.append([te1, te2])

        wo_free_tok = {}
        for j in range(4):
            if j + 1 < 4:
                if j + 1 == 1:
                    w = [t_pe_dft_end]
                else:
                    w = [wo_free_tok[(j + 1) % 2]]
                wo_tok[j + 1] = DMA(POOL, wo_buf[(j + 1) % 2], wo_d[j + 1][:, 0:DC, :], s_wo[(j + 1) % 2], waits=w)
            for t in range(NT):
                b, ft = banks.alloc()
                for c in range(DC):
                    tk = MM(bank(b), yT[:, c, t * 128:(t + 1) * 128], wo_buf[j % 2][:, c, :],
                            c == 0, c == DC - 1,
                            waits=([wo_tok[j], t_yf_end, t_dve_p3_end] + ft) if c == 0 else (), sig=(c == DC - 1))
                sl = slice(j * 512, (j + 1) * 512)
                ta = VTT(x1[:, t, sl], bank(b), x1[:, t, sl], ALU.add, waits=[tk, t_xr])
                banks.release(b, ta)
                add_tok[(t, j)] = ta
                if j == 3:
                    i = t % 2
                    ss = stat[:, t:t + 1]
                    sdv = stat[:, 8 + t:9 + t]
                    rs = stat[:, 16 + t:17 + t]
                    w0 = [ta] + ([h2t_free[i]] if h2t_free[i] else []) + [t_pe_dft_end]
                    tss = ACTF(h2t[i], x1[:, t, :], AF.Square, waits=w0, accum=ss)
                    tr = rstd_chain(ss, sdv, rs, D, tss)
                    th = VSTT(h2t[i], x1[:, t, :], rs, gb, ALU.mult, ALU.mult, waits=[tr, tss, t_gb_ffn])
                    pend_tr.append((t, th))
                    if len(pend_tr) > 1:
                        emit_h2_transposes(*pend_tr.pop(0))
            wo_free_tok[j % 2] = tk
        t_pe_wo_end = tk
        while pend_tr:
            emit_h2_transposes(*pend_tr.pop(0))
        t_h2_dve_end = th
        t_gb_fin = DMA(SP, gb, gfin_d[:, 0:D], s_gb, waits=[t_h2_dve_end])

        def load_gu(fc):
            w = []
            if fc >= 4:
                w = [gu_free[fc - 4]]
            else:
                w = [t_pe_wo_end, h2t_free[0], h2t_free[1]]
            s0 = (fc % 4) * 2
            DMA(POOL, gu_buf[s0], wg_d[fc][:, 0:DC, :], s_gu[s0], waits=w)
            gu_tok[fc] = [(s_gu[s0], s_gu[s0].n), DMA(POOL, gu_buf[s0 + 1], wup_d[fc][:, 0:DC, :], s_gu[s0 + 1])]

        s_wd = [newsem() for _ in range(2)]
        wd_tok = {}
        wd_free = {}
        nwd = [0]

        def load_wd(q, j):
            idx = q * 4 + j
            w = []
            if idx >= 2:
                w = [wd_free[idx - 2]]
            else:
                w = [t_pe_wo_end]
            wd_tok[idx] = DMA(POOL, wd_buf[idx % 2], wd_d[q, j][:, 0:FQ, :], s_wd[idx % 2], waits=w)

        for fc in range(4):
            load_gu(fc)
        load_wd(0, 0)
        load_wd(0, 1)

        def h2_waits(nh):
            w = []
            for t in range(nh * 4, nh * 4 + 4):
                w += h2_tok[t]
            return w

        sg_free = [None, None]
        nsg = 0
        hid_free = {}
        s_out = newsem()
        out_tok = None
        for q in range(NQ):
            hid_tok = {}
            for f in range(FQ):
                fc = q * FQ + f
                for nh in range(2):
                    bg, ftg = banks.alloc()
                    for dc in range(DC):
                        tkg = MM(bank(bg), gu_buf[(fc % 4) * 2][:, dc, :], h2T[:, dc, nh * 512:(nh + 1) * 512],
                                 dc == 0, dc == DC - 1,
                                 waits=([gu_tok[fc][0]] + (h2_waits(nh) if q == 0 else []) + ftg) if dc == 0 else (),
                                 sig=(dc == DC - 1))
                    bu, ftu = banks.alloc()
                    for dc in range(DC):
                        tku = MM(bank(bu), gu_buf[(fc % 4) * 2 + 1][:, dc, :], h2T[:, dc, nh * 512:(nh + 1) * 512],
                                 dc == 0, dc == DC - 1,
                                 waits=([gu_tok[fc][1]] + ftu) if dc == 0 else (), sig=(dc == DC - 1))
                    i = nsg % 2
                    nsg += 1
                    w = [tkg] + ([sg_free[i]] if sg_free[i] else [])
                    ts = ACTF(sgtmp[i], bank(bg), AF.Silu, waits=w)
                    banks.release(bg, ts)
                    w = [ts, tku] + ([hid_free[f]] if f in hid_free else [])
                    tm = VTT(hid(f)[:, nh * 512:(nh + 1) * 512], bank(bu), sgtmp[i], ALU.mult, waits=w)
                    banks.release(bu, tm)
                    sg_free[i] = tm
                    hid_tok[(f, nh)] = tm
                gu_free[fc] = tku
                if fc + 4 < NFC:
                    load_gu(fc + 4)
            for j in range(4):
                idx = q * 4 + j
                for t in range(NT):
                    b, ft = banks.alloc()
                    nh = t // 4
                    for f in range(FQ):
                        tk = MM(bank(b), hid(f)[:, t * 128:(t + 1) * 128], wd_buf[idx % 2][:, f, :],
                                f == 0, f == FQ - 1,
                                waits=([hid_tok[(f, nh)]] + ([wd_tok[idx]] + ft if f == 0 else [])),
                                sig=(f == FQ - 1))
                    sl = slice(j * 512, (j + 1) * 512)
                    ta = VTT(x1[:, t, sl], bank(b), x1[:, t, sl], ALU.add, waits=[tk, add_tok[(t, 3)]])
                    banks.release(b, ta)
                    if q == NQ - 1 and j == 3:
                        ss = stat[:, t:t + 1]
                        sdv = stat[:, 8 + t:9 + t]
                        rs = stat[:, 16 + t:17 + t]
                        tss = ACTF(junkC, x1[:, t, :], AF.Square, waits=[ta, tku], accum=ss)
                        tr = rstd_chain(ss, sdv, rs, D, tss)
                        tf = VSTT(x1[:, t, :], x1[:, t, :], rs, gb, ALU.mult, ALU.mult, waits=[tr, tss, t_gb_fin])
                        out_tok = DMA(SP, out_d[t * 128:(t + 1) * 128, :], x1[:, t, :], s_out, waits=[tf])
                wd_free[idx] = tk
                if idx + 2 < NQ * 4:
                    load_wd((idx + 2) // 4, (idx + 2) % 4)
            for f in range(FQ):
                hid_free[f] = tk
        SP.wait(out_tok)

        @block.tensor
        def _(e):
            PE.run(e)

        @block.scalar
        def _(e):
            ACT.run(e)

        @block.vector
        def _(e):
            DVE.run(e)

        @block.gpsimd
        def _(e):
            POOL.run(e)

        @block.sync
        def _(e):
            SP.run(e)

    return nc


_CACHE = {}


def _tables(c):
    kg = (512 * c + np.arange(512)).astype(np.int64)
    n = np.arange(2048, dtype=np.int64)
    ph = (n[:, None] * kg[None, :]) % SEQ
    ang = 2.0 * np.pi * ph.astype(np.float64) / SEQ
    sc = 1.0 / np.sqrt(SEQ)
    ct = (np.cos(ang) * sc).reshape(DC, P, 512).transpose(1, 0, 2)
    st = (-np.sin(ang) * sc).reshape(DC, P, 512).transpose(1, 0, 2)
    sg = (np.where(kg % 2 == 0, 1.0, -1.0) * sc).reshape(1, 512)
    bf = ml_dtypes.bfloat16
    return (np.ascontiguousarray(ct).astype(bf), np.ascontiguousarray(st).astype(bf), sg.astype(bf))


def kernel(x, norm_mix, w_in, w_fourier, sgu_norm, w_spatial, b_spatial, w_out,
           norm_ffn, w_gate, w_up, w_down, norm_final):
    f32 = np.float32
    bf = ml_dtypes.bfloat16
    x = np.asarray(x, f32)
    w_in = np.asarray(w_in, f32)[0]
    w_out = np.asarray(w_out, f32)[0]
    w_gate = np.asarray(w_gate, f32)[0]
    w_up = np.asarray(w_up, f32)[0]
    w_down = np.asarray(w_down, f32)[0]

    def lhs_tiles(w, ncol_chunks):
        return np.ascontiguousarray(w.reshape(DC, P, ncol_chunks, 128).transpose(2, 1, 0, 3))

    shared = {}
    shared["wa"] = lhs_tiles(w_in[:, 0:1024], 8)
    shared["wu"] = lhs_tiles(w_in[:, 1024:2048], 8)
    shared["wv"] = np.ascontiguousarray(w_in[:, 2048:3072].reshape(DC, P, 1024).transpose(1, 0, 2))
    shared["wo"] = np.ascontiguousarray(w_out.reshape(DC, P, 4, 512).transpose(2, 1, 0, 3))
    shared["wg"] = lhs_tiles(w_gate, NFC)
    shared["wup"] = lhs_tiles(w_up, NFC)
    shared["wd"] = np.ascontiguousarray(w_down.reshape(NQ, FQ, P, 4, 512).transpose(0, 3, 2, 1, 4))
    shared["wf"] = np.ascontiguousarray(np.asarray(w_fourier, f32)[0].transpose(1, 0, 2))
    shared["wst"] = np.ascontiguousarray(np.asarray(w_spatial, f32)[0].transpose(2, 0, 1))
    shared["bb"] = np.ascontiguousarray(np.broadcast_to(np.asarray(b_spatial, f32)[0][None], (P, 8, 128)))
    shared["sg"] = np.ascontiguousarray(np.asarray(sgu_norm, f32)[0].reshape(8, P).T)
    shared["gmix"] = np.ascontiguousarray(np.broadcast_to(np.asarray(norm_mix, f32)[0][None], (P, D)))
    shared["gffn"] = np.ascontiguousarray(np.broadcast_to(np.asarray(norm_ffn, f32)[0][None], (P, D)))
    shared["gfin"] = np.ascontiguousarray(np.broadcast_to(np.asarray(norm_final, f32)[None], (P, D)))
    dd = np.arange(128, dtype=np.int64)
    angd = 2.0 * np.pi * ((dd[:, None] * dd[None, :]) % 128).astype(np.float64) / 128
    shared["cd"] = (np.cos(angd) / np.sqrt(128.0)).astype(bf)
    shared["sd"] = (np.sin(angd) / np.sqrt(128.0)).astype(bf)
    shared["ident"] = np.eye(128).astype(bf)

    pad_axis = {"wa": 2, "wu": 2, "wv": 1, "wo": 2, "wg": 2, "wup": 2, "wd": 3, "gmix": 1, "gffn": 1, "gfin": 1}
    pad_len = {"gmix": 8, "gffn": 8, "gfin": 8}
    padded = {}
    for k, ax in pad_axis.items():
        a = shared.pop(k)
        pw = [(0, 0)] * a.ndim
        pw[ax] = (0, pad_len.get(k, 1))
        padded[k] = (np.pad(a, pw), ax, a.shape[ax])

    in_maps = []
    for c in range(NCORES):
        m = dict(shared)
        for k, (a, ax, n0) in padded.items():
            a = a.copy()
            idx = [0] * a.ndim
            idx[ax] = n0
            a[tuple(idx)] = float(c + 1)
            m[k] = a
        m["x"] = np.ascontiguousarray(np.concatenate([x[0, 512 * c:512 * c + 512], x[1, 512 * c:512 * c + 512]], 0))
        ct, st, sg = _tables(c)
        m["ctab"], m["nstab"], m["sgn"] = ct, st, sg
        in_maps.append(m)

    if "nc" not in _CACHE:
        _CACHE["nc"] = build_nc()
    nc = _CACHE["nc"]
    res = run_bass_kernel_spmd(nc, in_maps, core_ids=list(range(NCORES)))
    out = np.empty((2, SEQ, D), f32)
    for c in range(NCORES):
        o = np.asarray(res.results[c]["out"], f32)
        out[0, 512 * c:512 * c + 512] = o[0:512]
        out[1, 512 * c:512 * c + 512] = o[512:1024]
    return out
```

```python
import numpy as np
import ml_dtypes
import concourse.bass as bass
import concourse.mybir as mybir
from concourse.bass_utils import run_bass_kernel_spmd

F32 = mybir.dt.float32
BF16 = mybir.dt.bfloat16
AF = mybir.ActivationFunctionType
ALU = mybir.AluOpType

NCORES = 8
P = 128
D = 2048
DC = 16
SEQ = 4096
TOK = 1024
NT = 8
TB = 512
DFF = 5632
NFC = 44
NQ = 4
FQ = 11
EPS = 1e-6
KB = 1024


class Sem:
    def __init__(self, h):
        self.h = h
        self.n = 0


class EngQ:
    def __init__(self, name):
        self.name = name
        self.ops = []
        self.sem = None
        self.waited = {}

    def wait(self, tok):
        if tok is None:
            return
        sem, val = tok
        if self.waited.get(id(sem), 0) >= val:
            return
        self.waited[id(sem)] = val
        self.ops.append(lambda e: e.wait_ge(sem.h, val))

    def op(self, fn, waits=(), sig=False, inc=None):
        for w in waits:
            self.wait(w)
        if sig:
            sem = self.sem
            sem.n += 1
            v = sem.n
            self.ops.append(lambda e: fn(e).then_inc(sem.h, 1))
            return (sem, v)
        if inc is not None:
            inc.n += 16
            v = inc.n
            self.ops.append(lambda e: fn(e).then_inc(inc.h, 16))
            return (inc, v)
        self.ops.append(fn)
        return None

    def run(self, e):
        for f in self.ops:
            f(e)


class Banks:
    def __init__(self):
        self.next = 0
        self.free = [[] for _ in range(8)]

    def alloc(self, n=1):
        if n == 2 and self.next % 2:
            self.next = (self.next + 1) % 8
        b = self.next
        self.next = (self.next + n) % 8
        toks = []
        for i in range(n):
            toks += self.free[b + i]
            self.free[b + i] = []
        return b, toks

    def release(self, b, tok, n=1):
        for i in range(n):
            self.free[b + i].append(tok)


def build_nc():
    nc = bass.Bass("TRN2", target_bir_lowering=False)

    def din(name, shape, dt=F32):
        return nc.dram_tensor(name, list(shape), dt, kind="ExternalInput").ap()

    x_d = din("x", [TOK, D])
    gmix_d = din("gmix", [P, D + 8])
    gffn_d = din("gffn", [P, D + 8])
    gfin_d = din("gfin", [P, D + 8])
    wa_d = din("wa", [8, P, DC + 1, 128])
    wu_d = din("wu", [8, P, DC + 1, 128])
    wv_d = din("wv", [P, DC + 1, 1024])
    wo_d = din("wo", [4, P, DC + 1, 512])
    wg_d = din("wg", [NFC, P, DC + 1, 128])
    wup_d = din("wup", [NFC, P, DC + 1, 128])
    wd_d = din("wd", [NQ, 4, P, FQ + 1, 512])
    wf_d = din("wf", [P, 8, 128])
    wst_d = din("wst", [P, 8, 128])
    bb_d = din("bb", [P, 8, 128])
    sg_d = din("sg", [P, 8])
    ctab_d = din("ctab", [P, DC, 512], BF16)
    nstab_d = din("nstab", [P, DC, 512], BF16)
    sgn_d = din("sgn", [1, 512], BF16)
    cd_d = din("cd", [P, 128], BF16)
    sd_d = din("sd", [P, 128], BF16)
    id_d = din("ident", [P, 128], BF16)
    out_d = nc.dram_tensor("out", [TOK, D], F32, kind="ExternalOutput").ap()

    ARENA = 200 * KB
    n_sems = 52
    from contextlib import ExitStack
    with ExitStack() as es:
        arena = es.enter_context(nc.sbuf_tensor("arena", [P, ARENA // 2], BF16))
        psum = es.enter_context(nc.psum_tensor("psum", [P, 4096], F32))
        sems = [Sem(es.enter_context(nc.semaphore(f"s{i}"))) for i in range(n_sems)]
        block = es.enter_context(nc.Block())
        sem_it = iter(sems)

        def newsem():
            return next(sem_it)

        PE, ACT, DVE, POOL, SP = EngQ("pe"), EngQ("act"), EngQ("dve"), EngQ("pool"), EngQ("sp")
        PE.sem, ACT.sem, DVE.sem = newsem(), newsem(), newsem()
        cc_sem = newsem()

        def vb(off, shape):
            n = int(np.prod(shape))
            a = arena[:, off // 2: off // 2 + n]
            if len(shape) == 2:
                a = a.rearrange("p (a b) -> p a b", a=shape[0])
            elif len(shape) == 3:
                a = a.rearrange("p (a b c) -> p a b c", a=shape[0], b=shape[1])
            return a

        def vf(off, shape):
            n = int(np.prod(shape))
            a = arena[:, off // 2: off // 2 + 2 * n].bitcast(F32)
            if len(shape) == 2:
                a = a.rearrange("p (a b) -> p a b", a=shape[0])
            return a

        def bank(b, n=1):
            return psum[:, b * 512:(b + n) * 512]

        def bank_bf(b, n=1):
            return psum[:, b * 512:(b + n) * 512].bitcast(BF16)

        hT = vb(0, [DC, TOK])
        uT = vb(32 * KB, [8, TOK])
        aT = vb(48 * KB, [8, TOK])
        vn = vb(48 * KB, [NT, 1024])
        x1 = vf(0, [NT, D])
        wfb = vb(32 * KB, [8, 128])
        xt = [vf(96 * KB + i * 8 * KB, [D]) for i in range(4)] + [vf(64 * KB + i * 8 * KB, [D]) for i in range(2)]
        ht = [vb(80 * KB + i * 4 * KB, [D]) for i in range(4)]
        vfs = [vf(64 * KB + i * 4 * KB, [1024]) for i in range(2)]
        sptmp = [vf(72 * KB + i * 2 * KB, [512]) for i in range(2)]
        yT = vb(64 * KB, [16, TOK])
        wd_buf = [vb(64 * KB + i * 11 * KB, [FQ, 512]) for i in range(2)]
        hidA = vb(86 * KB, [5, TOK])
        ctab = vb(96 * KB, [DC, 512])
        nstab = vb(112 * KB, [DC, 512])
        h2T = vb(96 * KB, [DC, TOK])
        junkC = vb(96 * KB, [D])
        au_buf = [vb(128 * KB + i * 4 * KB, [DC, 128]) for i in range(4)]
        wv = vb(144 * KB, [DC, 1024])
        abh = vb(128 * KB, [4096])
        apb = vb(136 * KB, [2048])
        amb = vb(140 * KB, [2048])
        Bp = [vb(144 * KB + i * 8 * KB, [DC, 128]) for i in range(2)]
        Bm = [vb(148 * KB + i * 8 * KB, [DC, 128]) for i in range(2)]
        wo_buf = [vb(160 * KB, [DC, 512]), vb(128 * KB, [DC, 512])]
        h2t = [vb(144 * KB + i * 4 * KB, [D]) for i in range(2)]
        gu_off = [152, 156, 128, 132, 136, 140, 144, 148]
        gu_buf = [vb(gu_off[i] * KB, [DC, 128]) for i in range(8)]
        hidB = vb(160 * KB, [6, TOK])
        sgtmp = [vf(172 * KB + i * 2 * KB, [512]) for i in range(2)]
        gb = vf(176 * KB, [D])
        o = 184 * KB
        ident = vb(o, [128]); o += 256
        cdt = vb(o, [128]); o += 256
        sdt = vb(o, [128]); o += 256
        Gflat = vb(o, [2048]); G = vb(o, [8, 2, 128]); o += 4096
        wst = vb(o, [8, 128]); o += 2048
        bbt = vf(o, [8, 128]); o += 4096
        sgt = vf(o, [8]); o += 32
        sgn = vb(o, [512]); o += 1024
        bhalf = [vb(o + i * 256, [128]) for i in range(2)]; o += 512
        stat = vf(o, [64]); o += 256
        epsb = vf(o, [1]); o += 32
        assert o <= ARENA

        def hid(f):
            return hidA[:, f, :] if f < 5 else hidB[:, f - 5, :]

        banks = Banks()

        def MM(out, lhsT, rhs, start, stop, waits=(), sig=False):
            return PE.op(lambda e: e.matmul(out, lhsT, rhs, start=start, stop=stop), waits, sig)

        def TR(out, in_, waits=(), sig=False):
            return PE.op(lambda e: e.transpose(out, in_, ident), waits, sig)

        def ACTF(out, in_, func, waits=(), sig=True, bias=None, scale=None, accum=None):
            kw = {}
            if bias is not None:
                kw["bias"] = bias
            if scale is not None:
                kw["scale"] = scale
            if accum is not None:
                kw["accum_out"] = accum
            return ACT.op(lambda e: e.activation(out, in_, func, **kw), waits, sig)

        def ACOPY(out, in_, waits=(), sig=True):
            return ACT.op(lambda e: e.copy(out, in_), waits, sig)

        def VCOPY(out, in_, waits=(), sig=True):
            return DVE.op(lambda e: e.tensor_copy(out, in_), waits, sig)

        def VTT(out, in0, in1, op, waits=(), sig=True):
            return DVE.op(lambda e: e.tensor_tensor(out, in0, in1, op), waits, sig)

        def VTS(out, in0, s1, s2, op0, op1=None, waits=(), sig=True):
            if op1 is None:
                return DVE.op(lambda e: e.tensor_scalar(out, in0, s1, None, op0), waits, sig)
            return DVE.op(lambda e: e.tensor_scalar(out, in0, s1, s2, op0, op1), waits, sig)

        def VSTT(out, in0, scalar, in1, op0, op1, waits=(), sig=True):
            return DVE.op(lambda e: e.scalar_tensor_tensor(out, in0, scalar, in1, op0, op1), waits, sig)

        def VRECIP(out, in_, waits=(), sig=True):
            return DVE.op(lambda e: e.reciprocal(out, in_), waits, sig)

        def DMA(q, out, in_, sem, waits=()):
            return q.op(lambda e: e.dma_start(out=out, in_=in_), waits, inc=sem)

        def rstd_chain(ss_ap, sd_ap, r_ap, n, tok_ss):
            t1 = ACTF(sd_ap, ss_ap, AF.Sqrt, waits=[tok_ss], bias=epsb[:, 0:1], scale=1.0 / n)
            return VRECIP(r_ap, sd_ap, waits=[t1])

        evac_flip = [0]

        def evac_copy(out, in_, waits):
            evac_flip[0] ^= 1
            if evac_flip[0]:
                return ACOPY(out, in_, waits)
            return VCOPY(out, in_, waits)

        s_const = newsem()
        for dst, src in ((ident, id_d), (gb, gmix_d[:, 0:D]), (cdt, cd_d), (sdt, sd_d), (bbt, bb_d), (sgt, sg_d)):
            t_const = DMA(SP, dst, src, s_const)
        t_const = DMA(SP, sgn[0:1, :], sgn_d, s_const)
        s_pc = newsem()
        DMA(POOL, wfb, wf_d, s_pc)
        t_pc = DMA(POOL, wst, wst_d, s_pc)
        t_eps = DVE.op(lambda e: e.memset(epsb, EPS), sig=True)

        s_au = [newsem() for _ in range(4)]
        au_tok = {}
        au_free = {}
        def load_au(i):
            src = wa_d[i % 8] if i < 16 else wu_d[i - 16]
            waits = [au_free[i - 4]] if i >= 4 else []
            au_tok[i] = DMA(POOL, au_buf[i % 4], src[:, 0:DC, :], s_au[i % 4], waits)
        for i in range(4):
            load_au(i)

        tG = []
        for hp in range(4):
            b, ft = banks.alloc()
            for hh in range(2):
                h = hp * 2 + hh
                for cs, tab in enumerate((cdt, sdt)):
                    col = (hh * 2 + cs) * 128
                    tk = MM(bank(b)[:, col:col + 128], tab, wfb[:, h, :], True, True,
                            waits=[t_const, t_pc] + ft, sig=(hh == 1 and cs == 1))
            te = ACOPY(Gflat[:, hp * 512:(hp + 1) * 512], bank(b), [tk])
            banks.release(b, te)
            tG.append(te)
        tG_all = tG[-1]

        NX, NH = 6, 4
        s_wv = newsem()
        t_wv_box = [None]
        s_xt = [newsem() for _ in range(NX)]
        xt_free = [[] for _ in range(NX)]
        ht_free = [None] * NH
        h_tok = {}
        tss_tok = {}
        hT_tok = {}

        def norm_part(t):
            i, k = t % NX, t % NH
            tx = DMA(SP, xt[i], x_d[t * 128:(t + 1) * 128, :], s_xt[i], waits=xt_free[i])
            if t == 5:
                for kq in range(4):
                    t_wv_box[0] = DMA(POOL, wv[:, kq * 4:(kq + 1) * 4, :], wv_d[:, kq * 4:(kq + 1) * 4, :], s_wv,
                                      waits=[tx])
            ss = stat[:, t:t + 1]
            sdv = stat[:, 8 + t:9 + t]
            rs = stat[:, 16 + t:17 + t]
            w0 = [tx, t_eps] + ([ht_free[k]] if ht_free[k] else [])
            tss = ACTF(ht[k], xt[i], AF.Square, waits=w0, accum=ss)
            tr = rstd_chain(ss, sdv, rs, D, tss)
            th = VSTT(ht[k], xt[i], rs, gb, ALU.mult, ALU.mult, waits=[tr, tss, t_const])
            xt_free[i] = [th, tss]
            h_tok[t] = th
            tss_tok[t] = tss

        def tr_part(t):
            k = t % NH
            b, ft = banks.alloc(2)
            pT = bank_bf(b, 2).rearrange("p (c k) -> p c k", c=16)
            for dc in range(DC):
                tk = TR(pT[:, dc, :], ht[k][:, dc * 128:(dc + 1) * 128],
                        waits=[h_tok[t], t_const] + ft, sig=(dc == DC - 1))
            ht_free[k] = tk
            te1 = ACOPY(hT[:, 0:8, t * 128:(t + 1) * 128], pT[:, 0:8, :], [tk, tG_all])
            te2 = VCOPY(hT[:, 8:16, t * 128:(t + 1) * 128], pT[:, 8:16, :], [tk, tG_all])
            banks.release(b, te1)
            banks.release(b + 1, te2)
            hT_tok[t] = [te1, te2]

        def hT_waits(nh):
            w = []
            for t in range(nh * 4, nh * 4 + 4):
                w += hT_tok[t]
            return w

        s_ao = [newsem() for _ in range(2)]
        cc_sems = [cc_sem, newsem()]
        ibs = [nc.dram_tensor(f"ib{k}", [1024, TB], BF16) for k in range(2)]
        obs = [nc.dram_tensor(f"ob{k}", [NCORES * 1024, TB], BF16) for k in range(2)]
        t_ao = [None, None]
        t_cc = [None, None]

        def aT_part(nh):
            evs = []
            for m in range(8):
                i = nh * 8 + m
                b, ft = banks.alloc()
                for dc in range(DC):
                    tk = MM(bank(b), au_buf[i % 4][:, dc, :], hT[:, dc, nh * 512:(nh + 1) * 512],
                            dc == 0, dc == DC - 1,
                            waits=([au_tok[i]] + hT_waits(nh) + ft) if dc == 0 else (), sig=(dc == DC - 1))
                te = evac_copy(aT[:, m, nh * 512:(nh + 1) * 512], bank(b), [tk])
                banks.release(b, te)
                evs.append(te)
                au_free[i] = tk
                if i + 4 < 24:
                    load_au(i + 4)
            t_ao[nh] = DMA(SP, ibs[nh].ap().rearrange("(m p) t -> p m t", p=P), aT[:, :, nh * 512:(nh + 1) * 512],
                           s_ao[nh], waits=evs[-2:])
            POOL.wait(t_ao[nh])
            ccs = cc_sems[nh]
            ccs.n += 1
            ibk, obk = ibs[nh], obs[nh]
            POOL.ops.append(lambda e: e.collective_compute(
                "AllGather", ALU.bypass, replica_groups=[list(range(NCORES))],
                ins=[ibk[:, :]], outs=[obk[:, :]]).then_inc(ccs.h, 1))
            t_cc[nh] = (ccs, 1)

        norm_part(0)
        norm_part(1)
        for t in range(4):
            tr_part(t)
            norm_part(t + 2)
        norm_part(6)
        norm_part(7)
        s_tab = newsem()
        DMA(SP, ctab, ctab_d, s_tab, waits=[h_tok[NT - 1], tss_tok[NT - 1]])
        t_tab = DMA(SP, nstab, nstab_d, s_tab)
        s_gb = newsem()
        t_gb_ffn = DMA(SP, gb, gffn_d[:, 0:D], s_gb, waits=[h_tok[NT - 1]])
        aT_part(0)
        for t in range(4, NT):
            tr_part(t)
        aT_part(1)

        uT_tok = {}
        for m in range(8):
            i = 16 + m
            for nh in range(2):
                b, ft = banks.alloc()
                for dc in range(DC):
                    tk = MM(bank(b), au_buf[i % 4][:, dc, :], hT[:, dc, nh * 512:(nh + 1) * 512],
                            dc == 0, dc == DC - 1,
                            waits=([au_tok[i]] + ft) if dc == 0 else (), sig=(dc == DC - 1))
                te = ACTF(uT[:, m, nh * 512:(nh + 1) * 512], bank(b), AF.Gelu_apprx_tanh, waits=[tk])
                banks.release(b, te)
                uT_tok[(m, nh)] = te
            au_free[i] = tk
            if i + 4 < 24:
                load_au(i + 4)
        t_pe_u_end = tk

        vfs_free = [None, None]
        vn_tok = []
        for t in range(NT):
            i = t % 2
            tg = []
            for nh in range(2):
                b, ft = banks.alloc()
                for dc in range(DC):
                    tk = MM(bank(b), hT[:, dc, t * 128:(t + 1) * 128], wv[:, dc, nh * 512:(nh + 1) * 512],
                            dc == 0, dc == DC - 1,
                            waits=([t_wv_box[0]] + ft) if dc == 0 else (), sig=(dc == DC - 1))
                w = [tk] + ([vfs_free[i]] if vfs_free[i] else [])
                te = ACTF(vfs[i][:, nh * 512:(nh + 1) * 512], bank(b), AF.Gelu_apprx_tanh, waits=w)
                banks.release(b, te)
                tg.append(te)
            ss = stat[:, 24 + t:25 + t]
            sdv = stat[:, 32 + t:33 + t]
            rs = stat[:, 40 + t:41 + t]
            tss = ACTF(vn[:, t, :], vfs[i], AF.Square, waits=[tg[1]] + t_ao, accum=ss)
            tr = rstd_chain(ss, sdv, rs, 1024, tss)
            tv = VTS(vn[:, t, :], vfs[i], rs, None, ALU.mult, waits=[tr, tss] + t_ao)
            vfs_free[i] = tv
            vn_tok.append(tv)
        t_pe_v_end = tk

        sp_free = [None, None]
        yg_tok = []
        nsp = 0
        for h in range(8):
            for tgp in range(2):
                b, ft = banks.alloc()
                for tt in range(4):
                    t = tgp * 4 + tt
                    tk = MM(bank(b)[:, tt * 128:(tt + 1) * 128], vn[:, t, h * 128:(h + 1) * 128], wst[:, h, :],
                            True, True, waits=[vn_tok[t], t_pc] + (ft if tt == 0 else []), sig=(tt == 3))
                i = nsp % 2
                nsp += 1
                w = [tk, t_const] + ([sp_free[i]] if sp_free[i] else [])
                bbv = bbt[:, h, :].unsqueeze(1).broadcast_to([P, 4, 128])
                t1 = VSTT(sptmp[i].rearrange("p (a b) -> p a b", a=4), bank(b).rearrange("p (a b) -> p a b", a=4),
                          sgt[:, h:h + 1], bbv, ALU.mult, ALU.add, waits=w)
                banks.release(b, t1)
                t2 = VTT(yT[:, 8 + h, tgp * 512:(tgp + 1) * 512], sptmp[i], uT[:, h, tgp * 512:(tgp + 1) * 512],
                         ALU.mult, waits=[t1, uT_tok[(h, tgp)]])
                sp_free[i] = t2
                yg_tok.append(t2)
        t_pe_sp_end = tk
        t_dve_p3_end = yg_tok[-1]

        s_wo = [newsem() for _ in range(2)]
        wo_tok = {}
        wo_tok[0] = DMA(POOL, wo_buf[0], wo_d[0][:, 0:DC, :], s_wo[0], waits=[t_pe_v_end])
        s_xr = newsem()
        for t in range(NT):
            t_xr = DMA(POOL, x1[:, t, :], x_d[t * 128:(t + 1) * 128, :], s_xr,
                       waits=[t_pe_sp_end, t_dve_p3_end])

        s_abh = newsem()
        obv = [obs[k].ap().rearrange("(r m p) t -> p m r t", r=NCORES, m=8, p=P) for k in range(2)]
        abh3 = abh.rearrange("p (r t) -> p r t", r=NCORES)
        pairs = [(b_, h_) for b_ in range(2) for h_ in range(8)]
        abh_free = []
        fold_tok = {}
        bh_tok = {}
        B_free = [[], []]
        yf_tok = {}

        def stage_load_fold(i):
            b_, h_ = pairs[i]
            w = [t_cc[b_], t_pe_sp_end, t_pe_u_end] + abh_free
            tl = DMA(SP, abh3, obv[b_][:, h_, :, :], s_abh, waits=w)
            w2 = [tl, t_am0] + ([fold_tok[i - 1][3]] if i > 0 else [])
            ta = VTT(apb[:, 1:2048], abh[:, 1:2048], abh[:, 4095:2048:-1], ALU.add, waits=w2)
            tb = VTT(amb[:, 1:2048], abh[:, 1:2048], abh[:, 4095:2048:-1], ALU.subtract)
            tc = VCOPY(apb[:, 0:1], abh[:, 0:1])
            fold_tok[i] = [ta, tb, tc, None, tl]

        def stage_gmm(i):
            b_, h_ = pairs[i]
            k = i % 2
            ta, tb, tc, _, tl = fold_tok[i]
            b, ft = banks.alloc()
            tk = MM(bank(b)[0:1, 0:128], abh[:, 2048:2049], G[:, h_, 0, :], True, True,
                    waits=[tl, tG_all] + ft, sig=True)
            te = ACOPY(bhalf[k][0:1, :], bank(b)[0:1, 0:128], [tk])
            banks.release(b, te)
            bh_tok[i] = [te, None]
            evs = []
            for (src, cs, dstB) in ((apb, 0, Bp[k]), (amb, 1, Bm[k])):
                for half in range(2):
                    b, ft = banks.alloc(2)
                    for c8 in range(8):
                        c = half * 8 + c8
                        tk = MM(bank(b, 2)[:, c8 * 128:(c8 + 1) * 128], src[:, c * 128:(c + 1) * 128],
                                G[:, h_, cs, :], True, True,
                                waits=([ta, tb, tc] + ft + B_free[k]) if c8 == 0 else (), sig=(c8 == 7))
                    te = evac_copy(dstB[:, half * 8:(half + 1) * 8, :],
                                   bank(b, 2).rearrange("p (a b) -> p a b", a=8), [tk])
                    banks.release(b, te, 2)
                    evs.append(te)
            B_free[k] = []
            fold_tok[i][3] = tk
            abh_free.clear()
            abh_free.extend([tk, ta, tb, tc])
            return evs

        def stage_dft(i, evs):
            b_, h_ = pairs[i]
            k = i % 2
            b, ft = banks.alloc()
            n = 0
            for (srcB, tab) in ((Bp[k], ctab), (Bm[k], nstab)):
                for c in range(DC):
                    MM(bank(b), srcB[:, c, :], tab[:, c, :], n == 0, False,
                       waits=(evs + [t_tab] + ft) if n == 0 else ())
                    n += 1
            tk = MM(bank(b), bhalf[k][0:1, :], sgn[0:1, :], False, True, waits=[bh_tok[i][0], t_const], sig=True)
            bh_tok[i][1] = tk
            B_free[k] = [tk]
            te = ACOPY(yT[:, h_, b_ * TB:(b_ + 1) * TB], bank(b), [tk, t_dve_p3_end])
            banks.release(b, te)
            yf_tok[i] = te

        t_am0 = DVE.op(lambda e: e.memset(amb[:, 0:1], 0.0), waits=[t_pe_u_end, t_pe_sp_end], sig=True)
        stage_load_fold(0)
        evs_prev = stage_gmm(0)
        for i in range(16):
            if i + 1 < 16:
                stage_load_fold(i + 1)
                evs_next = stage_gmm(i + 1)
            stage_dft(i, evs_prev)
            evs_prev = evs_next
        t_pe_dft_end = bh_tok[15][1]
        t_yf_end = yf_tok[15]

        s_gu = [newsem() for _ in range(8)]
        gu_tok = {}
        gu_free = {}
        h2_tok = []
        add_tok = {}
        t_pe_wo_end = None
        h2t_free = [None, None]
        pend_tr = []

        def emit_h2_transposes(t, th):
            i = t % 2
            b, ft = banks.alloc(2)
            pT = bank_bf(b, 2).rearrange("p (c k) -> p c k", c=16)[:, :, 0:128]
            for dc in range(DC):
                tk = TR(pT[:, dc, :], h2t[i][:, dc * 128:(dc + 1) * 128],
                        waits=([th] + ft) if dc == 0 else (), sig=(dc == DC - 1))
            h2t_free[i] = tk
            te1 = ACOPY(h2T[:, 0:8, t * 128:(t + 1) * 128], pT[:, 0:8, :], [tk, t_pe_dft_end])
            te2 = VCOPY(h2T[:, 8:16, t * 128:(t + 1) * 128], pT[:, 8:16, :], [tk, t_pe_dft_end])
            banks.release(b, te1)
            banks.release(b + 1, te2)
            h2_tok.append([te1, te2])

        wo_free_tok = {}
        for j in range(4):
            if j + 1 < 4:
                if j + 1 == 1:
                    w = [t_pe_dft_end]
                else:
                    w = [wo_free_tok[(j + 1) % 2]]
                wo_tok[j + 1] = DMA(POOL, wo_buf[(j + 1) % 2], wo_d[j + 1][:, 0:DC, :], s_wo[(j + 1) % 2], waits=w)
            for t in range(NT):
                b, ft = banks.alloc()
                for c in range(DC):
                    tk = MM(bank(b), yT[:, c, t * 128:(t + 1) * 128], wo_buf[j % 2][:, c, :],
                            c == 0, c == DC - 1,
                            waits=([wo_tok[j], t_yf_end, t_dve_p3_end] + ft) if c == 0 else (), sig=(c == DC - 1))
                sl = slice(j * 512, (j + 1) * 512)
                ta = VTT(x1[:, t, sl], bank(b), x1[:, t, sl], ALU.add, waits=[tk, t_xr])
                banks.release(b, ta)
                add_tok[(t, j)] = ta
                if j == 3:
                    i = t % 2
                    ss = stat[:, t:t + 1]
                    sdv = stat[:, 8 + t:9 + t]
                    rs = stat[:, 16 + t:17 + t]
                    w0 = [ta] + ([h2t_free[i]] if h2t_free[i] else []) + [t_pe_dft_end]
                    tss = ACTF(h2t[i], x1[:, t, :], AF.Square, waits=w0, accum=ss)
                    tr = rstd_chain(ss, sdv, rs, D, tss)
                    th = VSTT(h2t[i], x1[:, t, :], rs, gb, ALU.mult, ALU.mult, waits=[tr, tss, t_gb_ffn])
                    pend_tr.append((t, th))
                    if len(pend_tr) > 1:
                        emit_h2_transposes(*pend_tr.pop(0))
            wo_free_tok[j % 2] = tk
        t_pe_wo_end = tk
        while pend_tr:
            emit_h2_transposes(*pend_tr.pop(0))
        t_h2_dve_end = th
        t_gb_fin = DMA(SP, gb, gfin_d[:, 0:D], s_gb, waits=[t_h2_dve_end])

        def load_gu(fc):
            w = []
            if fc >= 4:
                w = [gu_free[fc - 4]]
            elif fc == 0:
                w = [t_pe_dft_end]
            else:
                w = [t_pe_wo_end, h2t_free[0], h2t_free[1]]
            s0 = (fc % 4) * 2
            DMA(POOL, gu_buf[s0], wg_d[fc][:, 0:DC, :], s_gu[s0], waits=w)
            gu_tok[fc] = [(s_gu[s0], s_gu[s0].n), DMA(POOL, gu_buf[s0 + 1], wup_d[fc][:, 0:DC, :], s_gu[s0 + 1])]

        s_wd = [newsem() for _ in range(2)]
        wd_tok = {}
        wd_free = {}
        nwd = [0]

        def load_wd(q, j):
            idx = q * 4 + j
            w = []
            if idx >= 2:
                w = [wd_free[idx - 2]]
            else:
                w = [t_pe_wo_end]
            wd_tok[idx] = DMA(POOL, wd_buf[idx % 2], wd_d[q, j][:, 0:FQ, :], s_wd[idx % 2], waits=w)

        load_gu(0)
        for fc in range(1, 4):
            load_gu(fc)
        load_wd(0, 0)
        load_wd(0, 1)

        def h2_waits(nh):
            w = []
            for t in range(nh * 4, nh * 4 + 4):
                w += h2_tok[t]
            return w

        sg_free = [None, None]
        nsg = 0
        hid_free = {}
        s_out = newsem()
        out_tok = None
        for q in range(NQ):
            hid_tok = {}
            for f in range(FQ):
                fc = q * FQ + f
                for nh in range(2):
                    bg, ftg = banks.alloc()
                    for dc in range(DC):
                        tkg = MM(bank(bg), gu_buf[(fc % 4) * 2][:, dc, :], h2T[:, dc, nh * 512:(nh + 1) * 512],
                                 dc == 0, dc == DC - 1,
                                 waits=([gu_tok[fc][0]] + (h2_waits(nh) if q == 0 else []) + ftg) if dc == 0 else (),
                                 sig=(dc == DC - 1))
                    bu, ftu = banks.alloc()
                    for dc in range(DC):
                        tku = MM(bank(bu), gu_buf[(fc % 4) * 2 + 1][:, dc, :], h2T[:, dc, nh * 512:(nh + 1) * 512],
                                 dc == 0, dc == DC - 1,
                                 waits=([gu_tok[fc][1]] + ftu) if dc == 0 else (), sig=(dc == DC - 1))
                    i = nsg % 2
                    nsg += 1
                    w = [tkg] + ([sg_free[i]] if sg_free[i] else [])
                    ts = ACTF(sgtmp[i], bank(bg), AF.Silu, waits=w)
                    banks.release(bg, ts)
                    w = [ts, tku] + ([hid_free[f]] if f in hid_free else [])
                    tm = VTT(hid(f)[:, nh * 512:(nh + 1) * 512], bank(bu), sgtmp[i], ALU.mult, waits=w)
                    banks.release(bu, tm)
                    sg_free[i] = tm
                    hid_tok[(f, nh)] = tm
                gu_free[fc] = tku
                if fc + 4 < NFC:
                    load_gu(fc + 4)
            for j in range(4):
                idx = q * 4 + j
                for t in range(NT):
                    b, ft = banks.alloc()
                    nh = t // 4
                    for f in range(FQ):
                        tk = MM(bank(b), hid(f)[:, t * 128:(t + 1) * 128], wd_buf[idx % 2][:, f, :],
                                f == 0, f == FQ - 1,
                                waits=([hid_tok[(f, nh)]] + ([wd_tok[idx]] + ft if f == 0 else [])),
                                sig=(f == FQ - 1))
                    sl = slice(j * 512, (j + 1) * 512)
                    ta = VTT(x1[:, t, sl], bank(b), x1[:, t, sl], ALU.add, waits=[tk, add_tok[(t, 3)]])
                    banks.release(b, ta)
                    if q == NQ - 1 and j == 3:
                        ss = stat[:, t:t + 1]
                        sdv = stat[:, 8 + t:9 + t]
                        rs = stat[:, 16 + t:17 + t]
                        tss = ACTF(junkC, x1[:, t, :], AF.Square, waits=[ta, tku], accum=ss)
                        tr = rstd_chain(ss, sdv, rs, D, tss)
                        tf = VSTT(x1[:, t, :], x1[:, t, :], rs, gb, ALU.mult, ALU.mult, waits=[tr, tss, t_gb_fin])
                        out_tok = DMA(SP, out_d[t * 128:(t + 1) * 128, :], x1[:, t, :], s_out, waits=[tf])
                wd_free[idx] = tk
                if idx + 2 < NQ * 4:
                    load_wd((idx + 2) // 4, (idx + 2) % 4)
            for f in range(FQ):
                hid_free[f] = tk
        SP.wait(out_tok)

        @block.tensor
        def _(e):
            PE.run(e)

        @block.scalar
        def _(e):
            ACT.run(e)

        @block.vector
        def _(e):
            DVE.run(e)

        @block.gpsimd
        def _(e):
            POOL.run(e)

        @block.sync
        def _(e):
            SP.run(e)

    return nc


_CACHE = {}


def _tables(c):
    kg = (512 * c + np.arange(512)).astype(np.int64)
    n = np.arange(2048, dtype=np.int64)
    ph = (n[:, None] * kg[None, :]) % SEQ
    ang = 2.0 * np.pi * ph.astype(np.float64) / SEQ
    sc = 1.0 / np.sqrt(SEQ)
    ct = (np.cos(ang) * sc).reshape(DC, P, 512).transpose(1, 0, 2)
    st = (-np.sin(ang) * sc).reshape(DC, P, 512).transpose(1, 0, 2)
    sg = (np.where(kg % 2 == 0, 1.0, -1.0) * sc).reshape(1, 512)
    bf = ml_dtypes.bfloat16
    return (np.ascontiguousarray(ct).astype(bf), np.ascontiguousarray(st).astype(bf), sg.astype(bf))


def kernel(x, norm_mix, w_in, w_fourier, sgu_norm, w_spatial, b_spatial, w_out,
           norm_ffn, w_gate, w_up, w_down, norm_final):
    f32 = np.float32
    bf = ml_dtypes.bfloat16
    x = np.asarray(x, f32)
    w_in = np.asarray(w_in, f32)[0]
    w_out = np.asarray(w_out, f32)[0]
    w_gate = np.asarray(w_gate, f32)[0]
    w_up = np.asarray(w_up, f32)[0]
    w_down = np.asarray(w_down, f32)[0]

    def lhs_tiles(w, ncol_chunks):
        return np.ascontiguousarray(w.reshape(DC, P, ncol_chunks, 128).transpose(2, 1, 0, 3))

    shared = {}
    shared["wa"] = lhs_tiles(w_in[:, 0:1024], 8)
    shared["wu"] = lhs_tiles(w_in[:, 1024:2048], 8)
    shared["wv"] = np.ascontiguousarray(w_in[:, 2048:3072].reshape(DC, P, 1024).transpose(1, 0, 2))
    shared["wo"] = np.ascontiguousarray(w_out.reshape(DC, P, 4, 512).transpose(2, 1, 0, 3))
    shared["wg"] = lhs_tiles(w_gate, NFC)
    shared["wup"] = lhs_tiles(w_up, NFC)
    shared["wd"] = np.ascontiguousarray(w_down.reshape(NQ, FQ, P, 4, 512).transpose(0, 3, 2, 1, 4))
    shared["wf"] = np.ascontiguousarray(np.asarray(w_fourier, f32)[0].transpose(1, 0, 2))
    shared["wst"] = np.ascontiguousarray(np.asarray(w_spatial, f32)[0].transpose(2, 0, 1))
    shared["bb"] = np.ascontiguousarray(np.broadcast_to(np.asarray(b_spatial, f32)[0][None], (P, 8, 128)))
    shared["sg"] = np.ascontiguousarray(np.asarray(sgu_norm, f32)[0].reshape(8, P).T)
    shared["gmix"] = np.ascontiguousarray(np.broadcast_to(np.asarray(norm_mix, f32)[0][None], (P, D)))
    shared["gffn"] = np.ascontiguousarray(np.broadcast_to(np.asarray(norm_ffn, f32)[0][None], (P, D)))
    shared["gfin"] = np.ascontiguousarray(np.broadcast_to(np.asarray(norm_final, f32)[None], (P, D)))
    dd = np.arange(128, dtype=np.int64)
    angd = 2.0 * np.pi * ((dd[:, None] * dd[None, :]) % 128).astype(np.float64) / 128
    shared["cd"] = (np.cos(angd) / np.sqrt(128.0)).astype(bf)
    shared["sd"] = (np.sin(angd) / np.sqrt(128.0)).astype(bf)
    shared["ident"] = np.eye(128).astype(bf)

    pad_axis = {"wa": 2, "wu": 2, "wv": 1, "wo": 2, "wg": 2, "wup": 2, "wd": 3, "gmix": 1, "gffn": 1, "gfin": 1}
    pad_len = {"gmix": 8, "gffn": 8, "gfin": 8}
    padded = {}
    for k, ax in pad_axis.items():
        a = shared.pop(k)
        pw = [(0, 0)] * a.ndim
        pw[ax] = (0, pad_len.get(k, 1))
        padded[k] = (np.pad(a, pw), ax, a.shape[ax])

    in_maps = []
    for c in range(NCORES):
        m = dict(shared)
        for k, (a, ax, n0) in padded.items():
            a = a.copy()
            idx = [0] * a.ndim
            idx[ax] = n0
            a[tuple(idx)] = float(c + 1)
            m[k] = a
        m["x"] = np.ascontiguousarray(np.concatenate([x[0, 512 * c:512 * c + 512], x[1, 512 * c:512 * c + 512]], 0))
        ct, st, sg = _tables(c)
        m["ctab"], m["nstab"], m["sgn"] = ct, st, sg
        in_maps.append(m)

    if "nc" not in _CACHE:
        _CACHE["nc"] = build_nc()
    nc = _CACHE["nc"]
    res = run_bass_kernel_spmd(nc, in_maps, core_ids=list(range(NCORES)))
    out = np.empty((2, SEQ, D), f32)
    for c in range(NCORES):
        o = np.asarray(res.results[c]["out"], f32)
        out[0, 512 * c:512 * c + 512] = o[0:512]
        out[1, 512 * c:512 * c + 512] = o[512:1024]
    return out
```
